# Optimizing a Trainium2 kernel written in Bass

```python
import math
import jax, jax.numpy as jnp
from jax import lax
import numpy as np

D_MODEL = 2048
BATCH = 2
SEQ = 8192
DEPTH = 2

POOL_WINDOWS = (2, 4, 8, 16)
POOL_GROUPS = 4
POOL_GROUP_DIM = D_MODEL // 8
POOL_WIDTH = POOL_GROUPS * POOL_GROUP_DIM

GDN_K_HEADS = 4
GDN_V_HEADS = 8
GDN_HEAD_DIM = 128
GDN_KEY_WIDTH = GDN_K_HEADS * GDN_HEAD_DIM
GDN_VAL_WIDTH = GDN_V_HEADS * GDN_HEAD_DIM
GDN_CONV_CH = 2 * GDN_KEY_WIDTH + GDN_VAL_WIDTH
GDN_CONV = 4
GDN_CHUNK = 64

CONF_WIDTH = D_MODEL // 2
CONF_CONV = 31

MLA_HEADS = 8
MLA_NOPE = 128
MLA_ROPE = 64
MLA_V = 128
MLA_Q_RANK = 512
MLA_KV_RANK = 512
ROPE_THETA = 10000.0
ATTN_BLOCK = 128

N_BRANCH = 4
FFN_DIM = 11 * D_MODEL // 4
FFN_CONV = 3
RMS_EPS = 1e-6
LN_EPS = 1e-5

IN_SPLITS = (
    POOL_WIDTH,
    GDN_KEY_WIDTH, GDN_KEY_WIDTH,
    GDN_VAL_WIDTH, GDN_VAL_WIDTH,
    GDN_V_HEADS, GDN_V_HEADS,
    2 * CONF_WIDTH,
    MLA_Q_RANK, MLA_KV_RANK,
    MLA_ROPE,
    N_BRANCH * D_MODEL,
)
IN_WIDTH = sum(IN_SPLITS)
IN_OFFSETS = tuple(int(o) for o in np.cumsum(IN_SPLITS)[:-1])

kernel_name = "hybrid_gated_pool_gdn_conformer_mla_block"


def rms_norm(x, g, eps=RMS_EPS):
    x32 = x.astype(jnp.float32)
    y = x32 * lax.rsqrt(jnp.mean(x32 * x32, axis=-1, keepdims=True) + eps)
    return (y * g.astype(jnp.float32)).astype(x.dtype)


def layer_norm(x, g, b, eps=LN_EPS):
    x32 = x.astype(jnp.float32)
    xc = x32 - jnp.mean(x32, axis=-1, keepdims=True)
    var = jnp.mean(xc * xc, axis=-1, keepdims=True)
    y = xc * lax.rsqrt(var + eps) * g.astype(jnp.float32) + b.astype(jnp.float32)
    return y.astype(x.dtype)


def l2norm(t):
    return t * lax.rsqrt(jnp.sum(t * t, axis=-1, keepdims=True) + 1e-6)


def causal_dwconv(x, w):
    width = w.shape[0]
    return lax.conv_general_dilated(
        x, w[:, None, :].astype(x.dtype), window_strides=(1,),
        padding=[(width - 1, 0)], dimension_numbers=('NWC', 'WIO', 'NWC'),
        feature_group_count=x.shape[-1])


def rope_cos_sin(positions):
    inv_freq = ROPE_THETA ** (-jnp.arange(0, MLA_ROPE, 2, dtype=jnp.float32) / MLA_ROPE)
    ang = positions.astype(jnp.float32)[..., None] * inv_freq
    return jnp.cos(ang), jnp.sin(ang)


def apply_rope(x, cos, sin):
    x32 = x.astype(jnp.float32)
    half = x.shape[-1] // 2
    x1, x2 = x32[..., :half], x32[..., half:]
    return jnp.concatenate([x1 * cos - x2 * sin, x2 * cos + x1 * sin], axis=-1).astype(x.dtype)


def pool_mixer(u, w_groups, scale):
    b, s, _ = u.shape
    ug = u.reshape(b, s, POOL_GROUPS, POOL_GROUP_DIM)
    csum = jnp.cumsum(ug.astype(jnp.float32), axis=1)
    t = jnp.arange(s)
    pooled = []
    for gi, win in enumerate(POOL_WINDOWS):
        c = csum[:, :, gi]
        lag = jnp.pad(c, ((0, 0), (win, 0), (0, 0)))[:, :s]
        cnt = jnp.minimum(t + 1, win).astype(jnp.float32)[None, :, None]
        pooled.append((c - lag) / cnt)
    diff = (jnp.stack(pooled, axis=2) - ug.astype(jnp.float32)).astype(u.dtype)
    y = jnp.einsum('bsgc,gcd->bsgd', diff, w_groups)
    return y.reshape(b, s, POOL_WIDTH) * scale


def chunk_gated_delta_rule(q, k, v, g, beta):
    b, s, h, dk = q.shape
    dv = v.shape[-1]
    c = GDN_CHUNK
    n = s // c

    def to_chunks(t):
        t = jnp.moveaxis(t.astype(jnp.float32), 2, 1)
        return t.reshape((b, h, n, c) + t.shape[3:])

    q = to_chunks(l2norm(q.astype(jnp.float32)) * (dk ** -0.5))
    k = to_chunks(l2norm(k.astype(jnp.float32)))
    v = to_chunks(v)
    beta = to_chunks(beta)
    g = jnp.cumsum(to_chunks(g), axis=-1)

    lower = jnp.tril(jnp.ones((c, c), dtype=bool))
    strict = jnp.tril(jnp.ones((c, c), dtype=bool), -1)
    gdiff = g[..., :, None] - g[..., None, :]
    decay = jnp.where(lower, jnp.exp(jnp.where(lower, gdiff, 0.0)), 0.0)

    k_beta = k * beta[..., None]
    v_beta = v * beta[..., None]
    lmat = jnp.where(strict, jnp.einsum('bhnid,bhnjd->bhnij', k_beta, k) * decay, 0.0)
    amat = lmat + jnp.eye(c, dtype=jnp.float32)
    rhs = jnp.concatenate([v_beta, k_beta * jnp.exp(g)[..., None]], axis=-1)
    sol = lax.linalg.triangular_solve(amat, rhs, left_side=True, lower=True,
                                      unit_diagonal=True)
    u, w = sol[..., :dv], sol[..., dv:]
    qk = jnp.einsum('bhnid,bhnjd->bhnij', q, k) * decay

    def step(state, xs):
        q_i, k_i, u_i, w_i, g_i, qk_i = xs
        v_new = u_i - jnp.einsum('bhck,bhkv->bhcv', w_i, state)
        o_i = (jnp.einsum('bhck,bhkv->bhcv', q_i * jnp.exp(g_i)[..., None], state)
               + jnp.einsum('bhij,bhjv->bhiv', qk_i, v_new))
        g_last = g_i[..., -1:]
        state = (state * jnp.exp(g_last)[..., None]
                 + jnp.einsum('bhck,bhcv->bhkv', k_i * jnp.exp(g_last - g_i)[..., None], v_new))
        return state, o_i

    xs = tuple(jnp.moveaxis(t, 2, 0) for t in (q, k, u, w, g, qk))
    state0 = jnp.zeros((b, h, dk, dv), jnp.float32)
    _, o = lax.scan(step, state0, xs)
    o = jnp.moveaxis(o, 0, 2).reshape(b, h, s, dv)
    return jnp.moveaxis(o, 1, 2)


def gated_deltanet(q, k, v, z, a, bb, conv_w, a_log, dt_bias, norm_g):
    b, s, _ = q.shape
    qkv = jax.nn.silu(causal_dwconv(jnp.concatenate([q, k, v], axis=-1), conv_w))
    q, k, v = jnp.split(qkv, [GDN_KEY_WIDTH, 2 * GDN_KEY_WIDTH], axis=-1)
    rep = GDN_V_HEADS // GDN_K_HEADS
    q = jnp.repeat(q.reshape(b, s, GDN_K_HEADS, GDN_HEAD_DIM), rep, axis=2)
    k = jnp.repeat(k.reshape(b, s, GDN_K_HEADS, GDN_HEAD_DIM), rep, axis=2)
    v = v.reshape(b, s, GDN_V_HEADS, GDN_HEAD_DIM)
    beta = jax.nn.sigmoid(bb.astype(jnp.float32))
    g = -jnp.exp(a_log.astype(jnp.float32)) * jax.nn.softplus(
        a.astype(jnp.float32) + dt_bias.astype(jnp.float32))
    o = chunk_gated_delta_rule(q, k, v, g, beta)
    zf = z.reshape(b, s, GDN_V_HEADS, GDN_HEAD_DIM).astype(jnp.float32)
    o = rms_norm(o, norm_g) * jax.nn.silu(zf)
    return o.reshape(b, s, GDN_VAL_WIDTH).astype(z.dtype)


def conformer_conv(u, conv_w, conv_b, ln_g, ln_b):
    a, gate = jnp.split(u, 2, axis=-1)
    h = a * jax.nn.sigmoid(gate)
    h = causal_dwconv(h, conv_w) + conv_b
    h = layer_norm(h, ln_g, ln_b)
    return jax.nn.silu(h)


def causal_block_attention(q, k, v):
    b, s, h, dqk = q.shape
    nb = s // ATTN_BLOCK
    scale = dqk ** -0.5
    qb = jnp.moveaxis(q.reshape(b, nb, ATTN_BLOCK, h, dqk), 1, 0)
    k_idx = jnp.arange(s)

    def one_block(args):
        q_i, bi = args
        sc = jnp.einsum('bqhd,bkhd->bhqk', q_i, k).astype(jnp.float32) * scale
        q_idx = bi * ATTN_BLOCK + jnp.arange(ATTN_BLOCK)
        sc = jnp.where(k_idx[None, :] <= q_idx[:, None], sc, -jnp.inf)
        p = jax.nn.softmax(sc, axis=-1).astype(v.dtype)
        return jnp.einsum('bhqk,bkhd->bqhd', p, v)

    o = lax.map(one_block, (qb, jnp.arange(nb)))
    return jnp.moveaxis(o, 0, 1).reshape(b, s, h, v.shape[-1])


def mla(c_q, c_kv, k_rope, cos, sin, q_norm, w_uq, kv_norm, w_ukv):
    b, s, _ = c_q.shape
    q = (rms_norm(c_q, q_norm) @ w_uq).reshape(b, s, MLA_HEADS, MLA_NOPE + MLA_ROPE)
    kv = (rms_norm(c_kv, kv_norm) @ w_ukv).reshape(b, s, MLA_HEADS, MLA_NOPE + MLA_V)
    q_pe = apply_rope(q[..., MLA_NOPE:], cos[:, :, None], sin[:, :, None])
    k_pe = apply_rope(k_rope, cos, sin)
    q = jnp.concatenate([q[..., :MLA_NOPE], q_pe], axis=-1)
    k = jnp.concatenate([kv[..., :MLA_NOPE],
                         jnp.broadcast_to(k_pe[:, :, None, :], (b, s, MLA_HEADS, MLA_ROPE))], axis=-1)
    o = causal_block_attention(q, k, kv[..., MLA_NOPE:])
    return o.reshape(b, s, MLA_HEADS * MLA_V)


def hybrid_mixer(xn, cos, sin, w_in, pool_w, pool_scale, gdn_conv_w, gdn_a_log, gdn_dt_bias,
                 gdn_norm, conf_conv_w, conf_conv_b, conf_ln_g, conf_ln_b, mla_q_norm,
                 mla_w_uq, mla_kv_norm, mla_w_ukv, w_pool_out, w_gdn_out, w_conf_out,
                 w_mla_out, w_out):
    b, s, _ = xn.shape
    proj = xn @ w_in
    (u_pool, q_g, k_g, v_g, z_g, a_g, b_g, u_conf, c_q, c_kv, k_rope,
     gate_logits) = jnp.split(proj, IN_OFFSETS, axis=-1)
    y_a = pool_mixer(u_pool, pool_w, pool_scale) @ w_pool_out
    y_b = gated_deltanet(q_g, k_g, v_g, z_g, a_g, b_g, gdn_conv_w, gdn_a_log,
                         gdn_dt_bias, gdn_norm) @ w_gdn_out
    y_c = conformer_conv(u_conf, conf_conv_w, conf_conv_b, conf_ln_g, conf_ln_b) @ w_conf_out
    y_d = mla(c_q, c_kv, k_rope, cos, sin, mla_q_norm, mla_w_uq, mla_kv_norm,
              mla_w_ukv) @ w_mla_out
    gates = jax.nn.sigmoid(gate_logits.astype(jnp.float32)).reshape(
        b, s, N_BRANCH, D_MODEL).astype(xn.dtype)
    merged = (gates[:, :, 0] * y_a + gates[:, :, 1] * y_b
              + gates[:, :, 2] * y_c + gates[:, :, 3] * y_d)
    return merged @ w_out


def conv_ffn(xn, w_up, conv_w, conv_b, w_down):
    h = causal_dwconv(xn @ w_up, conv_w) + conv_b
    gate, up = jnp.split(h, 2, axis=-1)
    return (jax.nn.silu(gate) * up) @ w_down


def setup_inputs(seed: int = 0) -> dict:
    key = jax.random.key(seed)
    ks = iter(jax.random.split(key, 40))
    L = DEPTH
    f32 = jnp.float32

    def nrm(shape, fan_in):
        return jax.random.normal(next(ks), shape, f32) * (fan_in ** -0.5)

    def gain(shape):
        return 1.0 + 0.02 * jax.random.normal(next(ks), shape, f32)

    def small(shape):
        return 0.02 * jax.random.normal(next(ks), shape, f32)

    x = jax.random.normal(next(ks), (BATCH, SEQ, D_MODEL), f32)
    offset = jax.random.randint(next(ks), (BATCH, 1), 0, 1024, jnp.int32)
    positions = offset + jnp.arange(SEQ, dtype=jnp.int32)[None, :]
    mix_norm = gain((L, D_MODEL))
    w_in = nrm((L, D_MODEL, IN_WIDTH), D_MODEL)
    pool_w = nrm((L, POOL_GROUPS, POOL_GROUP_DIM, POOL_GROUP_DIM), POOL_GROUP_DIM)
    pool_scale = gain((L, POOL_WIDTH))
    gdn_conv_w = nrm((L, GDN_CONV, GDN_CONV_CH), GDN_CONV)
    gdn_a_log = jnp.log(jax.random.uniform(next(ks), (L, GDN_V_HEADS), f32, 1.0, 16.0))
    dt = jnp.exp(jax.random.uniform(next(ks), (L, GDN_V_HEADS), f32,
                                    math.log(1e-3), math.log(1e-1)))
    gdn_dt_bias = dt + jnp.log(-jnp.expm1(-dt))
    gdn_norm = gain((L, GDN_HEAD_DIM))
    conf_conv_w = nrm((L, CONF_CONV, CONF_WIDTH), CONF_CONV)
    conf_conv_b = small((L, CONF_WIDTH))
    conf_ln_g = gain((L, CONF_WIDTH))
    conf_ln_b = small((L, CONF_WIDTH))
    mla_q_norm = gain((L, MLA_Q_RANK))
    mla_w_uq = nrm((L, MLA_Q_RANK, MLA_HEADS * (MLA_NOPE + MLA_ROPE)), MLA_Q_RANK)
    mla_kv_norm = gain((L, MLA_KV_RANK))
    mla_w_ukv = nrm((L, MLA_KV_RANK, MLA_HEADS * (MLA_NOPE + MLA_V)), MLA_KV_RANK)
    w_pool_out = nrm((L, POOL_WIDTH, D_MODEL), POOL_WIDTH)
    w_gdn_out = nrm((L, GDN_VAL_WIDTH, D_MODEL), GDN_VAL_WIDTH)
    w_conf_out = nrm((L, CONF_WIDTH, D_MODEL), CONF_WIDTH)
    w_mla_out = nrm((L, MLA_HEADS * MLA_V, D_MODEL), MLA_HEADS * MLA_V)
    w_out = nrm((L, D_MODEL, D_MODEL), D_MODEL)
    ffn_norm = gain((L, D_MODEL))
    ffn_w_up = nrm((L, D_MODEL, 2 * FFN_DIM), D_MODEL)
    ffn_conv_w = nrm((L, FFN_CONV, 2 * FFN_DIM), FFN_CONV)
    ffn_conv_b = small((L, 2 * FFN_DIM))
    ffn_w_down = nrm((L, FFN_DIM, D_MODEL), FFN_DIM)
    final_norm = gain((D_MODEL,))
    return {
        'x': x, 'positions': positions, 'mix_norm': mix_norm, 'w_in': w_in,
        'pool_w': pool_w, 'pool_scale': pool_scale, 'gdn_conv_w': gdn_conv_w,
        'gdn_a_log': gdn_a_log, 'gdn_dt_bias': gdn_dt_bias, 'gdn_norm': gdn_norm,
        'conf_conv_w': conf_conv_w, 'conf_conv_b': conf_conv_b, 'conf_ln_g': conf_ln_g,
        'conf_ln_b': conf_ln_b, 'mla_q_norm': mla_q_norm, 'mla_w_uq': mla_w_uq,
        'mla_kv_norm': mla_kv_norm, 'mla_w_ukv': mla_w_ukv, 'w_pool_out': w_pool_out,
        'w_gdn_out': w_gdn_out, 'w_conf_out': w_conf_out, 'w_mla_out': w_mla_out,
        'w_out': w_out, 'ffn_norm': ffn_norm, 'ffn_w_up': ffn_w_up,
        'ffn_conv_w': ffn_conv_w, 'ffn_conv_b': ffn_conv_b, 'ffn_w_down': ffn_w_down,
        'final_norm': final_norm,
    }


def reference(x, positions, mix_norm, w_in, pool_w, pool_scale, gdn_conv_w, gdn_a_log,
              gdn_dt_bias, gdn_norm, conf_conv_w, conf_conv_b, conf_ln_g, conf_ln_b,
              mla_q_norm, mla_w_uq, mla_kv_norm, mla_w_ukv, w_pool_out, w_gdn_out,
              w_conf_out, w_mla_out, w_out, ffn_norm, ffn_w_up, ffn_conv_w, ffn_conv_b,
              ffn_w_down, final_norm):
    cos, sin = rope_cos_sin(positions)
    for l in range(DEPTH):
        xn = rms_norm(x, mix_norm[l])
        x = x + hybrid_mixer(xn, cos, sin, w_in[l], pool_w[l], pool_scale[l], gdn_conv_w[l],
                             gdn_a_log[l], gdn_dt_bias[l], gdn_norm[l], conf_conv_w[l],
                             conf_conv_b[l], conf_ln_g[l], conf_ln_b[l], mla_q_norm[l],
                             mla_w_uq[l], mla_kv_norm[l], mla_w_ukv[l], w_pool_out[l],
                             w_gdn_out[l], w_conf_out[l], w_mla_out[l], w_out[l])
        hn = rms_norm(x, ffn_norm[l])
        x = x + conv_ffn(hn, ffn_w_up[l], ffn_conv_w[l], ffn_conv_b[l], ffn_w_down[l])
    return rms_norm(x, final_norm)
```

```python
import contextlib
import numpy as np
import concourse.bass as bass
import concourse.mybir as mybir
from concourse.bass_utils import run_bass_kernel_spmd

F32 = mybir.dt.float32
BF16 = mybir.dt.bfloat16
I32 = mybir.dt.int32
AF = mybir.ActivationFunctionType
ALU = mybir.AluOpType
AX = mybir.AxisListType


class Op:
    __slots__ = ("eng", "fn", "fns", "deps", "signal", "sigval", "chan", "doneval")

    def __init__(self, eng):
        self.eng = eng
        self.fn = None
        self.fns = None
        self.deps = ()
        self.signal = False
        self.sigval = 0
        self.chan = None
        self.doneval = 0


class Sched:
    def __init__(self, nc, same_eng_sync=True):
        self.nc = nc
        self.ops = []
        self.last_w = {}
        self.readers = {}
        self.chan_last = {}
        self.chan_cnt = {}
        self.same = same_eng_sync

    def _deps(self, reads, writes):
        d = set()
        for k in reads:
            w = self.last_w.get(k)
            if w is not None:
                d.add(w)
        for k in writes:
            w = self.last_w.get(k)
            if w is not None:
                d.add(w)
            r = self.readers.get(k)
            if r:
                d.update(r.values())
        return d

    def _commit(self, o, reads, writes):
        for k in reads:
            r = self.readers.setdefault(k, {})
            if o.chan is None:
                r[o.eng] = o
            else:
                r[("dma", o.chan)] = o
        for k in writes:
            self.last_w[k] = o
            self.readers[k] = {}

    def op(self, eng, fn, reads=(), writes=()):
        o = Op(eng)
        o.fn = fn
        o.deps = self._deps(reads, writes)
        self._commit(o, reads, writes)
        self.ops.append(o)
        return o

    def dma(self, queue, chan, fns, reads=(), writes=()):
        o = Op(queue)
        o.fns = list(fns)
        o.chan = chan
        d = self._deps(reads, writes)
        p = self.chan_last.get(chan)
        if p is not None:
            d.add(p)
        o.deps = d
        self.chan_last[chan] = o
        self.chan_cnt[chan] = self.chan_cnt.get(chan, 0) + len(o.fns)
        o.doneval = 16 * self.chan_cnt[chan]
        self._commit(o, reads, writes)
        self.ops.append(o)
        return o

    def mm(self, out, lhsT, rhs, start=True, stop=True, reads=(), writes=(), **kw):
        return self.op("pe", lambda e: e.matmul(out, lhsT, rhs, start=start, stop=stop, **kw), reads, writes)

    def tr(self, out, in_, ident, reads=(), writes=()):
        return self.op("pe", lambda e: e.transpose(out, in_, ident), reads, writes)

    def act(self, out, in_, func, reads=(), writes=(), **kw):
        return self.op("act", lambda e: e.activation(out, in_, func, **kw), reads, writes)

    def v(self, eng, name, *args, reads=(), writes=(), **kw):
        return self.op(eng, lambda e: getattr(e, name)(*args, **kw), reads, writes)

    def emit(self):
        nc = self.nc
        same = self.same
        for o in self.ops:
            for p in o.deps:
                if p.chan is None:
                    if p.eng == o.eng and (p.eng == "pe" or not same):
                        continue
                    p.signal = True
        cnt = {}
        for o in self.ops:
            if o.chan is None and o.signal:
                cnt[o.eng] = cnt.get(o.eng, 0) + 1
                o.sigval = cnt[o.eng]
        engs = ["pe", "act", "dve", "pool", "sp"]
        by_eng = {e: [] for e in engs}
        for o in self.ops:
            by_eng[o.eng].append(o)
        with contextlib.ExitStack() as st:
            engsem = {e: st.enter_context(nc.semaphore("s_" + e)) for e in engs}
            chansem = {c: st.enter_context(nc.semaphore("c_" + c)) for c in self.chan_cnt}
            block = st.enter_context(nc.Block())

            def run(ename, eng):
                seen = {}
                for o in by_eng[ename]:
                    waits = {}
                    for p in o.deps:
                        if p.chan is not None:
                            key, sem, val = "c_" + p.chan, chansem[p.chan], p.doneval
                        else:
                            if p.eng == ename and (ename == "pe" or not same):
                                continue
                            key, sem, val = "s_" + p.eng, engsem[p.eng], p.sigval
                        if val > waits.get(key, (None, 0))[1]:
                            waits[key] = (sem, val)
                    for key, (sem, val) in waits.items():
                        if seen.get(key, 0) < val:
                            eng.wait_ge(sem, val)
                            seen[key] = val
                    if o.chan is None:
                        if o.fn is not None:
                            ins = o.fn(eng)
                            if o.signal:
                                ins.then_inc(engsem[ename], 1)
                        elif o.signal:
                            raise RuntimeError("wait-only op cannot signal")
                    else:
                        for f in o.fns:
                            f(eng).then_inc(chansem[o.chan], 16)

            @block.tensor
            def _(e):
                run("pe", e)

            @block.scalar
            def _(e):
                run("act", e)

            @block.vector
            def _(e):
                run("dve", e)

            @block.gpsimd
            def _(e):
                run("pool", e)

            @block.sync
            def _(e):
                run("sp", e)
        return cnt

import contextlib, math
import numpy as np

D = 2048
KC = 16
N_POOL = 1024
N_QKVZ = 3072
N_CONF = 2048
EPS = 1e-6
PI = math.pi


def build_A(T):
    assert T % 512 == 0 or T in (128, 256)
    TT = min(512, T)
    NT = T // TT
    NS = T // 128
    nc = bass.Bass("TRN2", target_bir_lowering=False)

    def din(name, shape, dt=F32):
        return nc.dram_tensor(name, shape, dt, kind="ExternalInput").ap()

    def dout(name, shape, dt=F32):
        return nc.dram_tensor(name, shape, dt, kind="ExternalOutput").ap()

    x = din("x", [T, D])
    pos = din("pos", [1, T], I32)
    consts = din("consts", [128, 8])
    ident_in = din("ident", [128, 128])
    g_mix = din("g_mix", [128, KC])
    qn = din("qn", [128, 4])
    kvn = din("kvn", [128, 4])
    w_pool = din("w_pool", [D, N_POOL])
    w_qkvz = din("w_qkvz", [D, N_QKVZ])
    w_ab = din("w_ab", [D, 16])
    w_conf = din("w_conf", [D, N_CONF])
    w_cq = din("w_cq", [D, 512])
    w_ckv = din("w_ckv", [D, 512])
    w_kr = din("w_kr", [D, 128])
    o_pool = dout("o_pool", [N_POOL, T])
    o_qkvz = dout("o_qkvz", [N_QKVZ, T])
    o_ab = dout("o_ab", [T, 16])
    o_hglu = dout("o_hglu", [1024, T])
    o_cq = dout("o_cq", [512, T], BF16)
    o_ckv = dout("o_ckv", [512, T], BF16)
    o_kpe = dout("o_kpe", [64, T], BF16)

    S = Sched(nc)
    with contextlib.ExitStack() as st:
        def sb(name, shape, dt=F32):
            return st.enter_context(nc.sbuf_tensor(name, shape, dt))

        def ps(name, shape, dt=F32):
            return st.enter_context(nc.psum_tensor(name, shape, dt))

        xnT = sb("xnT", [128, KC, T], BF16)
        wst = [sb("wst0", [128, KC, 256])]
        wbf = [sb(f"wbf{i}", [128, KC, 256], BF16) for i in range(2)]
        xt = [sb(f"xt{i}", [128, D]) for i in range(2)]
        xs = sb("xs", [128, D], BF16)
        junk = sb("junk", [128, D], BF16)
        ss = sb("ss", [128, NS])
        rstd = sb("rstd", [128, NS])
        cst = sb("cst", [128, 8])
        gm = sb("gm", [128, KC])
        qn_t = sb("qn_t", [128, 4])
        kvn_t = sb("kvn_t", [128, 4])
        idf = sb("idf", [128, 128])
        idb = sb("idb", [128, 128], BF16)
        onesf = sb("onesf", [128, 128])
        epsT = sb("epsT", [128, 1])
        ost = [sb(f"ost{i}", [128, T]) for i in range(3)]
        obf = [sb(f"obf{i}", [128, T], BF16) for i in range(2)]
        stash = sb("stash", [128, 4, T])
        sq = sb("sq", [128, TT])
        rb = sb("rb", [128, T])
        abw = sb("abw", [128, KC, 16], BF16)
        absb = sb("absb", [128, NS, 16])
        posi = stash[0:64, 0, :].bitcast(I32)
        ang = rb[0:64, :]
        cosT = stash[0:64, 1, :]
        sinT = stash[0:64, 2, :]
        P = [ps(f"P{i}", [128, 512]) for i in range(8)]
        rr = ost[2][0:64, :]
        kf = ost[0][0:64, :]
        msk = ost[1][0:64, :]
        ki = posi

        S.dma("sp", "cload", [
            lambda e: e.dma_start(out=cst[:], in_=consts[:, :]),
            lambda e: e.dma_start(out=gm[:], in_=g_mix[:, :]),
            lambda e: e.dma_start(out=qn_t[:], in_=qn[:, :]),
            lambda e: e.dma_start(out=kvn_t[:], in_=kvn[:, :]),
            lambda e: e.dma_start(out=idf[:], in_=ident_in[:, :]),
        ], writes=["cst", "gm", "qn_t", "kvn_t", "idf"])
        S.v("dve", "tensor_copy", idb[:], idf[:], reads=["idf"], writes=["idb"])
        S.v("pool", "memset", onesf[:], 1.0, writes=["onesf"])
        S.v("pool", "memset", epsT[:], EPS, writes=["epsT"])
        S.v("pool", "memset", ss[:], 0.0, writes=["ss"])

        for s in range(NS):
            xb = xt[s % 2]
            xk = f"xt{s % 2}"
            S.dma("sp", xk, [lambda e, xb=xb, s=s: e.dma_start(out=xb[:], in_=x[s * 128:(s + 1) * 128, :])], writes=[xk])
            S.act(junk[:], xb[:], AF.Square, accum_out=ss[:, s:s + 1], reads=[xk, "ss"], writes=["junk", "ss"])
            S.act(rstd[:, s:s + 1], ss[:, s:s + 1], AF.Sqrt, scale=1.0 / D, bias=epsT[:, 0:1], reads=["ss", "epsT"], writes=["rstd"])
            S.v("dve", "reciprocal", rstd[:, s:s + 1], rstd[:, s:s + 1], reads=["rstd"], writes=["rstd"])
            S.v("dve", "tensor_scalar", xs[:], xb[:], rstd[:, s:s + 1], None, ALU.mult, reads=[xk, "rstd"], writes=["xs"])
            for h in range(2):
                pt = P[(2 * s + h) % 8]
                pk = f"P{(2 * s + h) % 8}"
                pv = pt[:].bitcast(BF16)
                for j in range(8):
                    kc = h * 8 + j
                    S.tr(pv[:, j * 128:(j + 1) * 128], xs[:, kc * 128:(kc + 1) * 128], idb[:], reads=["xs", "idb"], writes=[pk])
                eng = "act" if h == 0 else "dve"
                dst = xnT[:, h * 8:(h + 1) * 8, s * 128:(s + 1) * 128]
                src = pv.rearrange("p (j t) -> p j t", j=8)
                if eng == "act":
                    S.act(dst, src, AF.Copy, reads=[pk], writes=["xnT"])
                else:
                    S.v("dve", "tensor_copy", dst, src, reads=[pk], writes=["xnT"])

        wctr = [0]

        def load_w(wdram, c0, ncols):
            i = wctr[0] % 2
            wctr[0] += 1
            S.dma("sp", "w0", [lambda e: e.dma_start(out=wst[0][:, :, 0:ncols],
                                                      in_=wdram[:, c0:c0 + ncols].rearrange("(kc p) c -> p kc c", p=128))],
                  writes=["wst0"])
            for kc in range(KC):
                if kc % 2 == 0:
                    S.act(wbf[i][:, kc, 0:ncols], wst[0][:, kc, 0:ncols], AF.Copy, scale=gm[:, kc:kc + 1],
                          reads=["wst0", "gm"], writes=[f"wbf{i}"])
                else:
                    S.v("dve", "tensor_scalar", wbf[i][:, kc, 0:ncols], wst[0][:, kc, 0:ncols], gm[:, kc:kc + 1], None, ALU.mult,
                        reads=["wst0", "gm"], writes=[f"wbf{i}"])
            return wbf[i], f"wbf{i}"

        pctr = [0]

        def project(wb, wk, coff, M):
            base = (pctr[0] % 2) * 4
            pctr[0] += 1
            outs = []
            for kc in range(KC):
                for tt in range(NT):
                    S.mm(P[base + tt][0:M, 0:TT], wb[:, kc, coff:coff + M], xnT[:, kc, tt * TT:(tt + 1) * TT],
                         start=(kc == 0), stop=(kc == KC - 1), reads=[wk, "xnT"], writes=[f"P{base + tt}"])
            for tt in range(NT):
                outs.append((P[base + tt], f"P{base + tt}"))
            return outs

        octr = [0]
        ectr = [0]

        def evac_copy(dst, src, reads, writes):
            ectr[0] += 1
            if ectr[0] % 2 == 0:
                S.act(dst, src, AF.Copy, reads=reads, writes=writes)
            else:
                S.v("dve", "tensor_copy", dst, src, reads=reads, writes=writes)

        def plain_chunks(wdram, ncols_total, odram):
            for c0 in range(0, ncols_total, 256):
                wb, wk = load_w(wdram, c0, 256)
                for j in range(2):
                    outs = project(wb, wk, j * 128, 128)
                    oi = octr[0] % 3
                    octr[0] += 1
                    for tt, (pt, pk) in enumerate(outs):
                        evac_copy(ost[oi][:, tt * TT:(tt + 1) * TT], pt[:, 0:TT], [pk], [f"ost{oi}"])
                    r0 = c0 + j * 128
                    S.dma("sp", f"o{oi}", [lambda e, oi=oi, r0=r0: e.dma_start(out=odram[r0:r0 + 128, :], in_=ost[oi][:])],
                          reads=[f"ost{oi}"], writes=[odram.name if hasattr(odram, "name") else "od"])

        plain_chunks(w_pool, N_POOL, o_pool)
        plain_chunks(w_qkvz, N_QKVZ, o_qkvz)

        for c in range(8):
            for half in range(2):
                wb, wk = load_w(w_conf, (c + 8 * half) * 128, 128)
                outs = project(wb, wk, 0, 128)
                if half == 0:
                    oi = octr[0] % 3
                    octr[0] += 1
                    for tt, (pt, pk) in enumerate(outs):
                        evac_copy(ost[oi][:, tt * TT:(tt + 1) * TT], pt[:, 0:TT], [pk], [f"ost{oi}"])
                else:
                    for tt, (pt, pk) in enumerate(outs):
                        S.act(rb[:, tt * TT:(tt + 1) * TT], pt[:, 0:TT], AF.Sigmoid, reads=[pk], writes=["rb"])
                    S.v("dve", "tensor_tensor", ost[oi][:], ost[oi][:], rb[:], ALU.mult, reads=[f"ost{oi}", "rb"], writes=[f"ost{oi}"])
                    S.dma("sp", f"o{oi}", [lambda e, oi=oi, c=c: e.dma_start(out=o_hglu[c * 128:(c + 1) * 128, :], in_=ost[oi][:])],
                          reads=[f"ost{oi}"], writes=["o_hglu"])

        for (wd, nt, od, oname) in ((w_cq, qn_t, o_cq, "o_cq"), (w_ckv, kvn_t, o_ckv, "o_ckv")):
            for c0 in range(0, 512, 256):
                wb, wk = load_w(wd, c0, 256)
                for j in range(2):
                    outs = project(wb, wk, j * 128, 128)
                    ci = c0 // 128 + j
                    for tt, (pt, pk) in enumerate(outs):
                        evac_copy(stash[:, ci, tt * TT:(tt + 1) * TT], pt[:, 0:TT], [pk], ["stash"])
            for tt in range(NT):
                pt, pk = P[tt], f"P{tt}"
                for ci in range(4):
                    S.act(sq[:], stash[:, ci, tt * TT:(tt + 1) * TT], AF.Square, reads=["stash"], writes=["sq"])
                    S.mm(pt[:, 0:TT], onesf[:], sq[:], start=(ci == 0), stop=(ci == 3), reads=["onesf", "sq"], writes=[pk])
                S.act(rb[:, tt * TT:(tt + 1) * TT], pt[:, 0:TT], AF.Sqrt, scale=1.0 / 512, bias=epsT[:, 0:1], reads=[pk, "epsT"], writes=["rb"])
            S.v("dve", "reciprocal", rb[:], rb[:], reads=["rb"], writes=["rb"])
            for ci in range(4):
                bi = ci % 2
                S.v("dve", "scalar_tensor_tensor", obf[bi][:], stash[:, ci, :], nt[:, ci:ci + 1], rb[:], ALU.mult, ALU.mult,
                    reads=["stash", "rb", "qn_t", "kvn_t"], writes=[f"obf{bi}"])
                S.dma("sp", f"ob{bi}", [lambda e, bi=bi, ci=ci, od=od: e.dma_start(out=od[ci * 128:(ci + 1) * 128, :], in_=obf[bi][:])],
                      reads=[f"obf{bi}"], writes=[oname])

        S.dma("sp", "pload", [lambda e: e.dma_start(out=posi[:], in_=pos[0:1, :].broadcast_to([64, T]))], writes=["stash"])
        S.v("dve", "tensor_copy", ang[:], posi[:], reads=["stash"], writes=["rb"])
        S.v("dve", "tensor_scalar", ang[:], ang[:], cst[0:64, 0:1], None, ALU.mult, reads=["rb", "cst"], writes=["rb"])

        def reduce_sin(dst, shift, scale_ap, tag):
            S.v("dve", "tensor_scalar", rr[:], ang[:], 1.0 / (2 * PI), (shift + PI) / (2 * PI), ALU.mult, ALU.add,
                reads=["rb"], writes=["ost2"])
            S.v("dve", "tensor_copy", ki[:], rr[:], reads=["ost2"], writes=["stash"])
            S.v("dve", "tensor_copy", kf[:], ki[:], reads=["stash"], writes=["ost0"])
            S.v("dve", "tensor_scalar", rr[:], ang[:], shift, None, ALU.add, reads=["rb"], writes=["ost2"])
            S.v("dve", "scalar_tensor_tensor", rr[:], kf[:], -2 * PI, rr[:], ALU.mult, ALU.add, reads=["ost0", "ost2"], writes=["ost2"])
            S.v("dve", "tensor_scalar", msk[:], rr[:], -PI, 2 * PI, ALU.is_lt, ALU.mult, reads=["ost2"], writes=["ost1"])
            S.v("dve", "tensor_tensor", rr[:], rr[:], msk[:], ALU.add, reads=["ost2", "ost1"], writes=["ost2"])
            S.v("dve", "tensor_scalar", msk[:], rr[:], PI, -2 * PI, ALU.is_gt, ALU.mult, reads=["ost2"], writes=["ost1"])
            S.v("dve", "tensor_tensor", rr[:], rr[:], msk[:], ALU.add, reads=["ost2", "ost1"], writes=["ost2"])
            S.v("dve", "tensor_scalar", rr[:], rr[:], 3.14159, -3.14159, ALU.min, ALU.max, reads=["ost2"], writes=["ost2"])
            if scale_ap is None:
                S.act(dst[:], rr[:], AF.Sin, reads=["ost2"], writes=[tag])
            else:
                S.act(dst[:], rr[:], AF.Sin, scale=scale_ap, reads=["ost2", "cst"], writes=[tag])

        reduce_sin(cosT, PI / 2, None, "stash")
        reduce_sin(sinT, 0.0, cst[0:64, 1:2], "stash")

        wb, wk = load_w(w_kr, 0, 128)
        o1 = project(wb, wk, 0, 64)
        o2 = project(wb, wk, 64, 64)
        for tt in range(NT):
            sl = slice(tt * TT, (tt + 1) * TT)
            S.v("dve", "tensor_tensor", rr[:, sl], o1[tt][0][0:64, 0:TT], cosT[:, sl], ALU.mult, reads=[o1[tt][1], "stash"], writes=["ost2"])
            S.v("dve", "tensor_tensor", kf[:, sl], o2[tt][0][0:64, 0:TT], sinT[:, sl], ALU.mult, reads=[o2[tt][1], "stash"], writes=["ost0"])
        S.v("dve", "tensor_tensor", obf[0][0:64, :], rr[:], kf[:], ALU.add, reads=["ost2", "ost0"], writes=["obf0"])
        S.dma("sp", "ob0", [lambda e: e.dma_start(out=o_kpe[:, :], in_=obf[0][0:64, :])], reads=["obf0"], writes=["o_kpe"])

        S.dma("sp", "w0", [lambda e: e.dma_start(out=wst[0][:, :, 0:16], in_=w_ab[:, :].rearrange("(kc p) c -> p kc c", p=128))],
              writes=["wst0"])
        for kc in range(KC):
            S.v("dve", "tensor_scalar", abw[:, kc, :], wst[0][:, kc, 0:16], gm[:, kc:kc + 1], None, ALU.mult,
                reads=["wst0", "gm"], writes=["abw"])
        for s in range(NS):
            pt, pk = P[s % 8], f"P{s % 8}"
            for kc in range(KC):
                S.mm(pt[:, 0:16], xnT[:, kc, s * 128:(s + 1) * 128], abw[:, kc, :], start=(kc == 0), stop=(kc == KC - 1),
                     reads=["xnT", "abw"], writes=[pk])
            S.v("dve", "tensor_copy", absb[:, s, :], pt[:, 0:16], reads=[pk], writes=["absb"])
        S.dma("sp", "oab", [lambda e: e.dma_start(out=o_ab[:, :].rearrange("(s p) c -> p s c", p=128), in_=absb[:])],
              reads=["absb"], writes=["o_ab"])

        S.op("sp", None, reads=["o_pool", "o_qkvz", "o_hglu", "o_cq", "o_ckv", "o_kpe", "o_ab", "od"])
        S.emit()
    return nc

import contextlib, math
import numpy as np

PI = math.pi
EPS = 1e-6
SCALE = 192 ** -0.5


def build_B(S, NB=2, do_gdn=True, do_mla=True):
    C = 128
    NCH = S // C
    SEG = min(512, S)
    NCS = SEG // C
    NSEG = S // SEG
    TQ = min(512, S)
    NTQ = S // TQ
    NT = NB * S
    nc = bass.Bass("TRN2", target_bir_lowering=False)

    def din(name, shape, dt=F32):
        return nc.dram_tensor(name, shape, dt, kind="ExternalInput").ap()

    def dout(name, shape, dt=F32):
        return nc.dram_tensor(name, shape, dt, kind="ExternalOutput").ap()

    gq = din("gq", [128, NT]); gk = din("gk", [128, NT]); gv = din("gv", [128, NT]); gz = din("gz", [128, NT])
    gab = din("gab", [NT, 2])
    gcw = din("gcw", [128, 12])
    gsc = din("gsc", [128, 4])
    masks = din("masks", [128, 5 * 128])
    cqn = din("cqn", [512, NT], BF16); ckvn = din("ckvn", [512, NT], BF16); kpe = din("kpe", [64, NT], BF16)
    wq = din("wq", [512, 256])
    wkv = din("wkv", [512, 256])
    pos = din("pos", [1, NT], I32)
    consts = din("consts", [128, 8])
    o_gdn = dout("o_gdn", [128, NT], BF16)
    o_mla = dout("o_mla", [128, NT], BF16)

    S_ = Sched(nc)
    with contextlib.ExitStack() as st:
        def sb(name, shape, dt=F32):
            return st.enter_context(nc.sbuf_tensor(name, shape, dt))

        def ps(name, shape, dt=F32):
            return st.enter_context(nc.psum_tensor(name, shape, dt))

        def dve(name, *a, r=(), w=(), **k):
            return S_.v("dve", name, *a, reads=r, writes=w, **k)

        def act(out, in_, func, r=(), w=(), **k):
            return S_.act(out, in_, func, reads=r, writes=w, **k)

        def mm(out, lhsT, rhs, start=True, stop=True, r=(), w=()):
            return S_.mm(out, lhsT, rhs, start=start, stop=stop, reads=r, writes=w)

        mks = [sb(f"mk{i}", [128, 128]) for i in range(5)]
        ident = mks[0][:, :]; triU = mks[1][:, :]; mSL = mks[2][:, :]; mU = mks[3][:, :]; mSU = mks[4][:, :]
        mkb = sb("mkb", [128, 2 * 128], BF16)
        identb = mkb[:, 0:128]; mUb = mkb[:, 128:256]
        cw = sb("cw", [128, 12]); sc = sb("sc", [128, 4]); cst = sb("cst", [128, 8])
        onesf = sb("onesf", [128, 128]); epsT = sb("epsT", [128, 1]); negA = sb("negA", [128, 1])
        S_.dma("sp", "cload", [
            *[(lambda e, i=i: e.dma_start(out=mks[i][:], in_=masks[:, i * 128:(i + 1) * 128])) for i in range(5)],
            lambda e: e.dma_start(out=cw[:], in_=gcw[:, :]),
            lambda e: e.dma_start(out=sc[:], in_=gsc[:, :]),
            lambda e: e.dma_start(out=cst[:], in_=consts[:, :]),
        ], writes=["mk", "cw", "sc", "cst"])
        dve("tensor_copy", mkb[:, 0:128], ident, r=["mk"], w=["mkb"])
        dve("tensor_copy", mkb[:, 128:256], mU, r=["mk"], w=["mkb"])
        S_.v("pool", "memset", onesf[:], 1.0, writes=["onesf"])
        S_.v("pool", "memset", epsT[:], EPS, writes=["epsT"])
        act(negA[:], sc[:, 0:1], AF.Exp, r=["sc"], w=["negA"])
        dve("tensor_scalar", negA[:], negA[:], -1.0, None, ALU.mult, r=["negA"], w=["negA"])

        PG = [ps(f"PG{i}", [128, 512]) for i in range(3)]
        PS = [ps(f"PS{i}", [128, 512]) for i in range(2)]
        PO = [ps(f"PO{i}", [128, 512]) for i in range(2)]
        PM = ps("PM", [128, 512])
        gslot = [0]

        def gps():
            i = gslot[0] % 3
            gslot[0] += 1
            return PG[i][:, 0:128], f"PG{i}"

        import os
        gst = [0]
        GSTOP = int(os.environ.get('GSTOP', '100000'))
        def stage_stop():
            gst[0] += 1
            return gst[0] >= GSTOP
        def gdn_stream(b):
            t0 = b * S
            raw = [sb(f"raw{b}_{i}", [128, 3 + SEG]) for i in range(3)]
            cv = [sb(f"cv{b}_{i}", [128, SEG]) for i in range(3)]
            zs = sb(f"zs{b}", [128, SEG]); tmp = sb(f"gtmp{b}", [128, SEG]); rin = sb(f"rin{b}", [128, SEG])
            og = sb(f"og{b}", [128, SEG], BF16)
            ab = sb(f"ab{b}", [128, NCS, 2])
            gg = sb(f"gg{b}", [128, NCS]); beta = sb(f"beta{b}", [128, NCS]); gc = sb(f"gc{b}", [128, NCS])
            gl = sb(f"gl{b}", [128, NCS]); eg = sb(f"eg{b}", [128, NCS]); egl = sb(f"egl{b}", [128, NCS])
            edl = sb(f"edl{b}", [128, NCS]); bge = sb(f"bge{b}", [128, NCS])
            St = sb(f"St{b}", [128, 128])
            dve("memset", St[:], 0.0, w=[f"St{b}"])
            NBUF = 2
            names = ["kb", "kbg", "kdec", "vb", "kbT", "Bm", "E", "decT", "decTs", "U", "L", "U2", "L2", "P", "P2",
                     "nwT", "qkT", "vnew", "oB", "otok", "on", "st1", "st2"]
            tl = {n: [sb(f"{n}{b}_{i}", [128, 128]) for i in range(NBUF)] for n in names}

            for sg in range(NSEG):
                s0 = t0 + sg * SEG
                K = lambda n: f"{n}{b}"
                fns = []
                for i, src in enumerate((gq, gk, gv)):
                    if sg == 0:
                        dve("memset", raw[i][:, 0:3], 0.0, w=[f"raw{b}_{i}"])
                        fns.append(lambda e, i=i, src=src, s0=s0: e.dma_start(out=raw[i][:, 3:3 + SEG], in_=src[:, s0:s0 + SEG]))
                    else:
                        fns.append(lambda e, i=i, src=src, s0=s0: e.dma_start(out=raw[i][:, 0:3 + SEG], in_=src[:, s0 - 3:s0 + SEG]))
                fns.append(lambda e, s0=s0: e.dma_start(out=zs[:], in_=gz[:, s0:s0 + SEG]))
                fns.append(lambda e, s0=s0: e.dma_start(out=ab[:], in_=gab[s0:s0 + SEG, :].rearrange("(n p) c -> p n c", p=128)))
                S_.dma("sp", f"gld{b}", fns, writes=[f"raw{b}_0", f"raw{b}_1", f"raw{b}_2", K("zs"), K("ab")])
                yield
                if stage_stop(): return
                for i in range(3):
                    rk = f"raw{b}_{i}"
                    dve("tensor_scalar", tmp[:], raw[i][:, 0:SEG], cw[:, 4 * i:4 * i + 1], None, ALU.mult, r=[rk, "cw"], w=[K("gtmp")])
                    for j in range(1, 4):
                        dve("scalar_tensor_tensor", tmp[:], raw[i][:, j:j + SEG], cw[:, 4 * i + j:4 * i + j + 1], tmp[:], ALU.mult, ALU.add,
                            r=[rk, "cw", K("gtmp")], w=[K("gtmp")])
                    act(cv[i][:], tmp[:], AF.Silu, r=[K("gtmp")], w=[f"cv{b}_{i}"])
                    yield
                    if stage_stop(): return
                act(zs[:], zs[:], AF.Silu, r=[K("zs")], w=[K("zs")])
                for i in range(2):
                    ck = f"cv{b}_{i}"
                    for t5 in range(0, SEG, 512):
                        w5 = min(512, SEG - t5)
                        act(tmp[:, t5:t5 + w5], cv[i][:, t5:t5 + w5], AF.Square, r=[ck], w=[K("gtmp")])
                        mm(PM[:, 0:w5], onesf[:], tmp[:, t5:t5 + w5], r=["onesf", K("gtmp")], w=["PM"])
                        act(rin[:, t5:t5 + w5], PM[:, 0:w5], AF.Sqrt, bias=epsT[:, 0:1], r=["PM", "epsT"], w=[K("rin")])
                    dve("reciprocal", rin[:], rin[:], r=[K("rin")], w=[K("rin")])
                    if i == 0:
                        dve("scalar_tensor_tensor", cv[i][:], cv[i][:], 128 ** -0.5, rin[:], ALU.mult, ALU.mult, r=[ck, K("rin")], w=[ck])
                    else:
                        dve("tensor_tensor", cv[i][:], cv[i][:], rin[:], ALU.mult, r=[ck, K("rin")], w=[ck])
                    yield
                    if stage_stop(): return
                act(gg[:], ab[:, :, 0], AF.Exp, bias=sc[:, 1:2], r=[K("ab"), "sc"], w=[K("gg")])
                act(gg[:], gg[:], AF.Ln, bias=1.0, r=[K("gg")], w=[K("gg")])
                dve("tensor_scalar", gg[:], gg[:], negA[:, 0:1], None, ALU.mult, r=[K("gg"), "negA"], w=[K("gg")])
                act(beta[:], ab[:, :, 1], AF.Sigmoid, r=[K("ab")], w=[K("beta")])
                mm(PM[:, 0:NCS], triU, gg[:], r=["mk", K("gg")], w=["PM"])
                dve("tensor_copy", gc[:], PM[:, 0:NCS], r=["PM"], w=[K("gc")])
                mm(PM[:, 0:NCS], onesf[:], gg[:], r=["onesf", K("gg")], w=["PM"])
                dve("tensor_copy", gl[:], PM[:, 0:NCS], r=["PM"], w=[K("gl")])
                act(eg[:], gc[:], AF.Exp, r=[K("gc")], w=[K("eg")])
                act(egl[:], gl[:], AF.Exp, r=[K("gl")], w=[K("egl")])
                dve("tensor_tensor", edl[:], gl[:], gc[:], ALU.subtract, r=[K("gl"), K("gc")], w=[K("edl")])
                act(edl[:], edl[:], AF.Exp, r=[K("edl")], w=[K("edl")])
                dve("tensor_tensor", bge[:], beta[:], eg[:], ALU.mult, r=[K("beta"), K("eg")], w=[K("bge")])
                yield
                if stage_stop(): return
                for n in range(NCS):
                    bi = n % NBUF
                    T = lambda nm: tl[nm][bi]
                    TK = lambda nm: f"{nm}{b}_{bi}"
                    cs = slice(n * C, (n + 1) * C)
                    qT = cv[0][:, cs]; kT = cv[1][:, cs]; vT = cv[2][:, cs]
                    qk_, kk_, vk_ = f"cv{b}_0", f"cv{b}_1", f"cv{b}_2"
                    col = lambda t_: t_[:, n:n + 1]
                    p1, p1k = gps()
                    S_.tr(p1, kT, ident, reads=[kk_, "mk"], writes=[p1k])
                    dve("tensor_scalar", T("kb")[:], p1, col(beta), None, ALU.mult, r=[p1k, K("beta")], w=[TK("kb")])
                    dve("tensor_scalar", T("kbg")[:], p1, col(bge), None, ALU.mult, r=[p1k, K("bge")], w=[TK("kbg")])
                    dve("tensor_scalar", T("kdec")[:], p1, col(edl), None, ALU.mult, r=[p1k, K("edl")], w=[TK("kdec")])
                    p2, p2k = gps()
                    S_.tr(p2, vT, ident, reads=[vk_, "mk"], writes=[p2k])
                    dve("tensor_scalar", T("vb")[:], p2, col(beta), None, ALU.mult, r=[p2k, K("beta")], w=[TK("vb")])
                    yield
                    if stage_stop(): return
                    GSUB = int(os.environ.get('GSUB', '99'))
                    p3, p3k = gps()
                    S_.tr(p3, T("kb")[:], ident, reads=[TK("kb"), "mk"], writes=[p3k])
                    if GSUB < 1: return
                    act(T("kbT")[:], p3, AF.Copy, r=[p3k], w=[TK("kbT")])
                    if GSUB < 2: return
                    act(T("Bm")[:], mSL, AF.Copy, scale=col(gg), r=["mk", K("gg")], w=[TK("Bm")])
                    if GSUB < 3: return
                    p4, p4k = gps()
                    mm(p4, T("Bm")[:], triU, r=[TK("Bm"), "mk"], w=[p4k])
                    if GSUB < 4: return
                    act(T("E")[:], p4, AF.Exp, r=[p4k], w=[TK("E")])
                    if GSUB < 5: return
                    dve("tensor_tensor", T("decT")[:], T("E")[:], mU, ALU.mult, r=[TK("E"), "mk"], w=[TK("decT")])
                    dve("tensor_tensor", T("decTs")[:], T("E")[:], mSU, ALU.mult, r=[TK("E"), "mk"], w=[TK("decTs")])
                    yield
                    if stage_stop(): return
                    G2 = int(os.environ.get('GSUB2', '99'))
                    p5, p5k = gps()
                    mm(p5, kT, T("kbT")[:], r=[kk_, TK("kbT")], w=[p5k])
                    if G2 < 1: return
                    dve("tensor_tensor", T("U")[:], p5, T("decTs")[:], ALU.mult, r=[p5k, TK("decTs")], w=[TK("U")])
                    if G2 < 2: return
                    p6, p6k = gps()
                    S_.tr(p6, T("U")[:], ident, reads=[TK("U"), "mk"], writes=[p6k])
                    act(T("L")[:], p6, AF.Copy, r=[p6k], w=[TK("L")])
                    if G2 < 3: return
                    dve("tensor_tensor", T("P")[:], ident, T("U")[:], ALU.subtract, r=["mk", TK("U")], w=[TK("P")])
                    if G2 < 4: return
                    p7, p7k = gps()
                    mm(p7, kT, qT, r=[kk_, qk_], w=[p7k])
                    if G2 < 5: return
                    dve("tensor_tensor", T("qkT")[:], p7, T("decT")[:], ALU.mult, r=[p7k, TK("decT")], w=[TK("qkT")])
                    yield
                    if stage_stop(): return
                    Uc, Lc, Pc = "U", "L", "P"
                    Un, Ln, Pn = "U2", "L2", "P2"
                    for step in range(6):
                        last = step == 5
                        pa, pak = gps()
                        mm(pa, T(Uc)[:], T(Lc)[:], r=[TK(Uc), TK(Lc)], w=[pak])
                        act(T(Ln)[:], pa, AF.Copy, r=[pak], w=[TK(Ln)])
                        if not last:
                            pb, pbk = gps()
                            mm(pb, T(Lc)[:], T(Uc)[:], r=[TK(Uc), TK(Lc)], w=[pbk])
                            dve("tensor_copy", T(Un)[:], pb, r=[pbk], w=[TK(Un)])
                        pc, pck = gps()
                        mm(pc, T(Ln)[:], T(Pc)[:], r=[TK(Ln), TK(Pc)], w=[pck])
                        dve("tensor_tensor", T(Pn)[:], pc, T(Pc)[:], ALU.add, r=[pck, TK(Pc)], w=[TK(Pn)])
                        Uc, Un = Un, Uc
                        Lc, Ln = Ln, Lc
                        Pc, Pn = Pn, Pc
                        yield
                        if stage_stop(): return
                    Tt, Ttk = T(Pc), TK(Pc)
                    p8, p8k = gps()
                    mm(p8, T("kbg")[:], Tt[:], r=[TK("kbg"), Ttk], w=[p8k])
                    act(T("nwT")[:], p8, AF.Copy, scale=-1.0, r=[p8k], w=[TK("nwT")])
                    yield
                    if stage_stop(): return
                    p9, p9k = gps()
                    mm(p9, Tt[:], T("vb")[:], start=True, stop=False, r=[Ttk, TK("vb")], w=[p9k])
                    mm(p9, T("nwT")[:], St[:], start=False, stop=True, r=[TK("nwT"), K("St")], w=[p9k])
                    act(T("vnew")[:], p9, AF.Copy, r=[p9k], w=[TK("vnew")])
                    pA, pAk = gps()
                    mm(pA, qT, St[:], r=[qk_, K("St")], w=[pAk])
                    pB, pBk = gps()
                    mm(pB, T("qkT")[:], T("vnew")[:], r=[TK("qkT"), TK("vnew")], w=[pBk])
                    act(T("oB")[:], pB, AF.Copy, r=[pBk], w=[TK("oB")])
                    dve("scalar_tensor_tensor", T("otok")[:], pA, col(eg), T("oB")[:], ALU.mult, ALU.add,
                        r=[pAk, K("eg"), TK("oB")], w=[TK("otok")])
                    pC, pCk = gps()
                    mm(pC, T("kdec")[:], T("vnew")[:], r=[TK("kdec"), TK("vnew")], w=[pCk])
                    dve("scalar_tensor_tensor", St[:], St[:], col(egl), pC, ALU.mult, ALU.add, r=[K("St"), K("egl"), pCk], w=[K("St")])
                    yield
                    if stage_stop(): return
                    dve("memset", T("st1")[:, 0:1], 0.0, w=[TK("st1")])
                    act(T("on")[:], T("otok")[:], AF.Square, accum_out=T("st1")[:, 0:1], r=[TK("otok")], w=[TK("on"), TK("st1")])
                    act(T("st1")[:, 0:1], T("st1")[:, 0:1], AF.Sqrt, scale=1.0 / 128, bias=epsT[:, 0:1], r=[TK("st1"), "epsT"], w=[TK("st1")])
                    dve("reciprocal", T("st1")[:, 0:1], T("st1")[:, 0:1], r=[TK("st1")], w=[TK("st1")])
                    dve("tensor_scalar", T("on")[:], T("otok")[:], T("st1")[:, 0:1], None, ALU.mult, r=[TK("otok"), TK("st1")], w=[TK("on")])
                    pD, pDk = gps()
                    S_.tr(pD, T("on")[:], ident, reads=[TK("on"), "mk"], writes=[pDk])
                    dve("scalar_tensor_tensor", og[:, cs], pD, sc[:, 2:3], zs[:, cs], ALU.mult, ALU.mult, r=[pDk, "sc", K("zs")], w=[K("og")])
                    yield
                    if stage_stop(): return
                S_.dma("sp", f"gst{b}", [lambda e, s0=s0: e.dma_start(out=o_gdn[:, s0:s0 + SEG], in_=og[:])], reads=[K("og")], writes=["o_gdn"])
                yield
                if stage_stop(): return

        def mla_stream():
            wst = sb("wst", [128, 4, 256]); wqb = sb("wqb", [128, 4, 256], BF16); wkvb = sb("wkvb", [128, 4, 256], BF16)
            S_.dma("sp", "wl", [lambda e: e.dma_start(out=wst[:], in_=wq[:, :].rearrange("(kc p) c -> p kc c", p=128))], writes=["wst"])
            dve("tensor_copy", wqb[:], wst[:], r=["wst"], w=["wqb"])
            S_.dma("sp", "wl", [lambda e: e.dma_start(out=wst[:], in_=wkv[:, :].rearrange("(kc p) c -> p kc c", p=128))], writes=["wst"])
            dve("tensor_copy", wkvb[:], wst[:], r=["wst"], w=["wkvb"])
            kT = sb("kT", [128, S], BF16); ka = sb("ka", [65, S], BF16); vt = sb("vt", [128, S // 128, 129], BF16)
            cq = [sb(f"cq{i}", [128, 4, TQ], BF16) for i in range(2)]
            ckv = [sb(f"ckv{i}", [128, 4, TQ], BF16) for i in range(2)]
            qTn = sb("qTn", [128, TQ], BF16); qa = sb("qa", [65, TQ], BF16)
            sqf = sb("sqf", [128, TQ]); sqr = sb("sqr", [64, TQ])
            sel = sb("sel", [128, 65]); kmax = sb("kmax", [65, 1]); kmt = sb("kmt", [65, 1]); mrow = sb("mrow", [65, TQ])
            pT = [sb(f"pT{i}", [128, TQ], BF16) for i in range(3)]
            onb = sb("onb", [128, 128], BF16); rcp = sb("rcp", [128, 1]); om = sb("om", [128, TQ], BF16)
            posi = sb("posi", [64, TQ], I32); ang = sb("ang", [64, TQ]); cosT = sb("cosT", [64, TQ]); sinT = sb("sinT", [64, TQ])
            rr = sb("rr", [64, TQ]); ki = sb("ki", [64, TQ], I32); kf = sb("kf", [64, TQ]); msk = sb("msk", [64, TQ])
            S_.v("pool", "memset", sel[:], 0.0, writes=["sel"])
            S_.v("pool", "memset", sel[:, 64:65], 1.0, writes=["sel"])
            dve("memset", vt[:, :, 128:129], 1.0, w=["vt"])
            dve("memset", ka[64:65, :], 1.0, w=["ka"])
            pctr = [0]

            def reduce_sin(dst, shift, scale_ap, tag):
                dve("tensor_scalar", rr[:], ang[:], 1.0 / (2 * PI), (shift + PI) / (2 * PI), ALU.mult, ALU.add, r=["ang"], w=["rr"])
                dve("tensor_copy", ki[:], rr[:], r=["rr"], w=["ki"])
                dve("tensor_copy", kf[:], ki[:], r=["ki"], w=["kf"])
                dve("tensor_scalar", rr[:], ang[:], shift, None, ALU.add, r=["ang"], w=["rr"])
                dve("scalar_tensor_tensor", rr[:], kf[:], -2 * PI, rr[:], ALU.mult, ALU.add, r=["kf", "rr"], w=["rr"])
                dve("tensor_scalar", msk[:], rr[:], -PI, 2 * PI, ALU.is_lt, ALU.mult, r=["rr"], w=["msk"])
                dve("tensor_tensor", rr[:], rr[:], msk[:], ALU.add, r=["rr", "msk"], w=["rr"])
                dve("tensor_scalar", msk[:], rr[:], PI, -2 * PI, ALU.is_gt, ALU.mult, r=["rr"], w=["msk"])
                dve("tensor_tensor", rr[:], rr[:], msk[:], ALU.add, r=["rr", "msk"], w=["rr"])
                dve("tensor_scalar", rr[:], rr[:], 3.14159, -3.14159, ALU.min, ALU.max, r=["rr"], w=["rr"])
                if scale_ap is None:
                    act(dst[:], rr[:], AF.Sin, r=["rr"], w=[tag])
                else:
                    act(dst[:], rr[:], AF.Sin, scale=scale_ap, r=["rr", "cst"], w=[tag])

            for b in range(NB):
                t0 = b * S
                dve("memset", kmax[64:65, :], 0.0, w=["kmax"])
                for tq in range(NTQ):
                    q0 = tq * TQ
                    g0 = t0 + q0
                    li = (b * NTQ + tq) % 2
                    S_.dma("sp", f"lat{li}", [
                        lambda e, li=li, g0=g0: e.dma_start(out=cq[li][:], in_=cqn[:, g0:g0 + TQ].rearrange("(kc p) t -> p kc t", p=128)),
                        lambda e, li=li, g0=g0: e.dma_start(out=ckv[li][:], in_=ckvn[:, g0:g0 + TQ].rearrange("(kc p) t -> p kc t", p=128)),
                    ], writes=[f"cq{li}", f"ckv{li}"])
                    S_.dma("sp", "kpl", [
                        lambda e, g0=g0, q0=q0: e.dma_start(out=ka[0:64, q0:q0 + TQ], in_=kpe[:, g0:g0 + TQ]),
                        lambda e, g0=g0: e.dma_start(out=posi[:], in_=pos[0:1, g0:g0 + TQ].broadcast_to([64, TQ])),
                    ], writes=["ka", "posi"])
                    dve("tensor_copy", ang[:], posi[:], r=["posi"], w=["ang"])
                    dve("tensor_scalar", ang[:], ang[:], cst[0:64, 0:1], None, ALU.mult, r=["ang", "cst"], w=["ang"])
                    reduce_sin(cosT, PI / 2, None, "cosT")
                    reduce_sin(sinT, 0.0, cst[0:64, 1:2], "sinT")
                    yield
                    for kc in range(4):
                        mm(PM[:, 0:TQ], wkvb[:, kc, 0:128], ckv[li][:, kc, :], start=(kc == 0), stop=(kc == 3), r=["wkvb", f"ckv{li}"], w=["PM"])
                    act(kT[:, q0:q0 + TQ], PM[:, 0:TQ], AF.Copy, r=["PM"], w=["kT"])
                    act(sqf[:], PM[:, 0:TQ], AF.Square, r=["PM"], w=["sqf"])
                    dve("tensor_tensor", sqr[:], ka[0:64, q0:q0 + TQ], ka[0:64, q0:q0 + TQ], ALU.mult, r=["ka"], w=["sqr"])
                    mm(PM[0:65, 0:TQ], sel[:, :], sqf[:], start=True, stop=False, r=["sel", "sqf"], w=["PM"])
                    mm(PM[0:65, 0:TQ], sel[0:64, :], sqr[:], start=False, stop=True, r=["sel", "sqr"], w=["PM"])
                    dve("reduce_max", kmt[64:65, :], PM[64:65, 0:TQ], AX.X, r=["PM"], w=["kmt"])
                    dve("tensor_tensor", kmax[64:65, :], kmax[64:65, :], kmt[64:65, :], ALU.max, r=["kmax", "kmt"], w=["kmax"])
                    yield
                    for sbk in range(TQ // 128):
                        for kc in range(4):
                            mm(PM[:, 0:128], ckv[li][:, kc, sbk * 128:(sbk + 1) * 128], wkvb[:, kc, 128:256], start=(kc == 0), stop=(kc == 3),
                               r=["wkvb", f"ckv{li}"], w=["PM"])
                        dve("tensor_copy", vt[:, q0 // 128 + sbk, 0:128], PM[:, 0:128], r=["PM"], w=["vt"])
                    yield
                    for kc in range(4):
                        mm(PM[:, 0:TQ], wqb[:, kc, 0:128], cq[li][:, kc, :], start=(kc == 0), stop=(kc == 3), r=["wqb", f"cq{li}"], w=["PM"])
                    act(qTn[:], PM[:, 0:TQ], AF.Copy, r=["PM"], w=["qTn"])
                    act(sqf[:], PM[:, 0:TQ], AF.Square, r=["PM"], w=["sqf"])
                    for kc in range(4):
                        mm(PM[0:64, 0:TQ], wqb[:, kc, 128:192], cq[li][:, kc, :], start=(kc == 0), stop=(kc == 3), r=["wqb", f"cq{li}"], w=["PM"])
                    dve("tensor_tensor", rr[:], PM[0:64, 0:TQ], cosT[:], ALU.mult, r=["PM", "cosT"], w=["rr"])
                    for kc in range(4):
                        mm(PM[0:64, 0:TQ], wqb[:, kc, 192:256], cq[li][:, kc, :], start=(kc == 0), stop=(kc == 3), r=["wqb", f"cq{li}"], w=["PM"])
                    dve("tensor_tensor", kf[:], PM[0:64, 0:TQ], sinT[:], ALU.mult, r=["PM", "sinT"], w=["kf"])
                    dve("tensor_tensor", rr[:], rr[:], kf[:], ALU.add, r=["rr", "kf"], w=["rr"])
                    dve("tensor_copy", qa[0:64, :], rr[:], r=["rr"], w=["qa"])
                    dve("tensor_tensor", sqr[:], rr[:], rr[:], ALU.mult, r=["rr"], w=["sqr"])
                    mm(PM[0:65, 0:TQ], sel[:, :], sqf[:], start=True, stop=False, r=["sel", "sqf"], w=["PM"])
                    mm(PM[0:65, 0:TQ], sel[0:64, :], sqr[:], start=False, stop=True, r=["sel", "sqr"], w=["PM"])
                    dve("tensor_scalar", mrow[64:65, :], PM[64:65, 0:TQ], kmax[64:65, 0:1], None, ALU.mult, r=["PM", "kmax"], w=["mrow"])
                    act(mrow[64:65, :], mrow[64:65, :], AF.Sqrt, r=["mrow"], w=["mrow"])
                    dve("tensor_scalar", qa[64:65, :], mrow[64:65, :], -1.0, None, ALU.mult, r=["mrow"], w=["qa"])
                    yield
                    nkb = (q0 + TQ) // 128
                    nqb = TQ // 128
                    first = [True, True]
                    for kb in range(nkb):
                        n0 = max(0, kb * 128 - q0)
                        N = TQ - n0
                        pi_ = pctr[0] % 2
                        pti = pctr[0] % 3
                        pctr[0] += 1
                        sp_, spk = PS[pi_], f"PS{pi_}"
                        mm(sp_[:, 0:N], kT[:, kb * 128:(kb + 1) * 128], qTn[:, n0:TQ], start=True, stop=False, r=["kT", "qTn"], w=[spk])
                        mm(sp_[:, 0:N], ka[0:65, kb * 128:(kb + 1) * 128], qa[0:65, n0:TQ], start=False, stop=True, r=["ka", "qa"], w=[spk])
                        act(pT[pti][:, 0:N], sp_[:, 0:N], AF.Exp, scale=SCALE, r=[spk], w=[f"pT{pti}"])
                        if kb * 128 >= q0:
                            dve("tensor_tensor", pT[pti][:, 0:128], pT[pti][:, 0:128], mUb, ALU.mult, r=[f"pT{pti}", "mkb"], w=[f"pT{pti}"])
                        for qb in range(n0 // 128, nqb):
                            bank = qb // 2
                            oc = (qb % 2) * 129
                            c0 = qb * 128 - n0
                            lastkb = (q0 + qb * 128) // 128
                            S_.mm(PO[bank][:, oc:oc + 129], pT[pti][:, c0:c0 + 128], vt[:, kb, :], start=first[bank],
                                  stop=(kb == lastkb and qb % 2 == 1), reads=[f"pT{pti}", "vt"], writes=[f"PO{bank}"], skip_group_check=True)
                            first[bank] = False
                        if kb % 2 == 1:
                            yield
                    for qb in range(nqb):
                        bank = qb // 2
                        oc = (qb % 2) * 129
                        dve("reciprocal", rcp[:], PO[bank][:, oc + 128:oc + 129], r=[f"PO{bank}"], w=["rcp"])
                        dve("tensor_scalar", onb[:], PO[bank][:, oc:oc + 128], rcp[:, 0:1], None, ALU.mult, r=[f"PO{bank}", "rcp"], w=["onb"])
                        pv = PM[:].bitcast(BF16)
                        S_.tr(pv[:, 0:128], onb[:], identb, reads=["onb", "mkb"], writes=["PM"])
                        act(om[:, qb * 128:(qb + 1) * 128], pv[:, 0:128], AF.Copy, r=["PM"], w=["om"])
                    S_.dma("sp", "mst", [lambda e, g0=g0: e.dma_start(out=o_mla[:, g0:g0 + TQ], in_=om[:])], reads=["om"], writes=["o_mla"])
                    yield

        gens = []
        if do_gdn:
            gens += [gdn_stream(b) for b in range(NB)]
        if do_mla:
            gens.append(mla_stream())
        while gens:
            for g in list(gens):
                try:
                    next(g)
                except StopIteration:
                    gens.remove(g)
        S_.op("sp", None, reads=["o_gdn", "o_mla"])
        S_.emit()
    return nc

import contextlib, math
import numpy as np

D = 2048
KC = 16
EPS = 1e-6
LN_EPS = 1e-5
HALO = 32
FFN = 5632
NFC = FFN // 128


class Ctx:
    def __init__(self, nc, st):
        self.nc = nc
        self.st = st
        self.S = Sched(nc)
        self.P = [st.enter_context(nc.psum_tensor(f"P{i}", [128, 512], F32)) for i in range(8)]
        self.pctr = 0
        self.ectr = 0

    def sb(self, name, shape, dt=F32):
        return self.st.enter_context(self.nc.sbuf_tensor(name, shape, dt))

    def pb(self):
        i = self.pctr % 8
        self.pctr += 1
        return self.P[i], f"P{i}"

    def dve(self, name, *a, r=(), w=(), **k):
        return self.S.v("dve", name, *a, reads=r, writes=w, **k)

    def act(self, out, in_, func, r=(), w=(), **k):
        return self.S.act(out, in_, func, reads=r, writes=w, **k)

    def mm(self, out, lhsT, rhs, start=True, stop=True, r=(), w=(), **k):
        return self.S.mm(out, lhsT, rhs, start=start, stop=stop, reads=r, writes=w, **k)

    def copy(self, dst, src, r, w):
        self.ectr += 1
        if self.ectr % 2 == 0:
            self.act(dst, src, AF.Copy, r=r, w=w)
        else:
            self.dve("tensor_copy", dst, src, r=r, w=w)


def norm_transpose(cx, xrows_ap, nrows, xt, xk, xs, junk, ss, rstd, epsT, idb, gm, dstT, col0, load=True):
    S = cx.S
    if load:
        S.dma("sp", xk, [lambda e: e.dma_start(out=xt[0:nrows, :], in_=xrows_ap)], writes=[xk])
    cx.dve("memset", ss[0:nrows, 0:1], 0.0, w=["ss"])
    cx.act(junk[0:nrows, :], xt[0:nrows, :], AF.Square, accum_out=ss[0:nrows, 0:1], r=[xk, "ss"], w=["junk", "ss"])
    cx.act(rstd[0:nrows, 0:1], ss[0:nrows, 0:1], AF.Sqrt, scale=1.0 / D, bias=epsT[0:nrows, 0:1], r=["ss", "epsT"], w=["rstd"])
    cx.dve("reciprocal", rstd[0:nrows, 0:1], rstd[0:nrows, 0:1], r=["rstd"], w=["rstd"])
    cx.dve("tensor_scalar", xs[0:nrows, :], xt[0:nrows, :], rstd[0:nrows, 0:1], None, ALU.mult, r=[xk, "rstd"], w=["xs"])
    for h in range(2):
        pt, pk = cx.pb()
        pv = pt[:].bitcast(BF16)
        for j in range(8):
            kc = h * 8 + j
            S.tr(pv[:, j * 128:j * 128 + nrows], xs[0:nrows, kc * 128:(kc + 1) * 128], idb[0:nrows, 0:nrows], reads=["xs", "idb"], writes=[pk])
        for j in range(8):
            kc = h * 8 + j
            src = pv[:, j * 128:j * 128 + nrows]
            dst = dstT[:, kc, col0:col0 + nrows]
            if j % 2 == 0:
                cx.act(dst, src, AF.Copy, scale=gm[:, kc:kc + 1], r=[pk, "gm"], w=["xnT"])
            else:
                cx.dve("tensor_scalar", dst, src, gm[:, kc:kc + 1], None, ALU.mult, r=[pk, "gm"], w=["xnT"])


def build_C1(T):
    TH = min(512, T)
    NTH = T // TH
    W = HALO + TH
    nc = bass.Bass("TRN2", target_bir_lowering=False)

    def din(name, shape, dt=F32):
        return nc.dram_tensor(name, shape, dt, kind="ExternalInput").ap()

    x = din("x", [T, D])
    upool = din("upool", [1024, HALO + T])
    hglu = din("hglu", [1024, HALO + T])
    ogdn = din("ogdn", [1024, T], BF16)
    omla = din("omla", [1024, T], BF16)
    ident_in = din("ident", [128, 128])
    g_mix = din("g_mix", [128, KC])
    cvec = din("cvec", [128, 64])
    cconv = din("cconv", [128, 8 * 31])
    poolw = din("poolw", [8, 128, 256])
    wg = din("wg", [64, 128, KC * 128])
    wo = din("wo", [64, 128, 8 * 128])
    wout = din("wout", [16, 128, KC * 128])
    xmid = nc.dram_tensor("xmid", [T, D], F32, kind="ExternalOutput").ap()

    with contextlib.ExitStack() as st:
        cx = Ctx(nc, st)
        S = cx.S
        sb, dve, act, mm = cx.sb, cx.dve, cx.act, cx.mm
        xnT = sb("xnT", [128, KC, TH], BF16)
        br = [sb(f"br{i}", [128, 8, TH], BF16) for i in range(4)]
        merged = sb("merged", [128, KC, TH], BF16)
        wst = [sb(f"wst{i}", [128, KC * 128]) for i in range(2)]
        wbf = [sb(f"wbf{i}", [128, KC * 128], BF16) for i in range(2)]
        xt = [sb(f"xt{i}", [128, D]) for i in range(2)]
        xs = sb("xs", [128, D], BF16); junk = sb("junk", [128, D], BF16)
        ss = sb("ss", [128, 1]); rstd = sb("rstd", [128, 1]); epsT = sb("epsT", [128, 1]); lnepsT = sb("lnepsT", [128, 1])
        idf = sb("idf", [128, 128]); idb = sb("idb", [128, 128], BF16)
        gm = sb("gm", [128, KC]); cv = sb("cv", [128, 64]); ccv = sb("ccv", [128, 8 * 31])
        onesf = sb("onesf", [128, 128])
        pwst = sb("pwst", [128, 8, 256]); pwb = sb("pwb", [128, 8, 256], BF16)
        ub = [sb(f"ub{i}", [128, W]) for i in range(2)]
        s1 = sb("s1", [128, W]); s2 = sb("s2", [128, W])
        dT = sb("dT", [128, 8, TH], BF16)
        invc = sb("invc", [128, 4, 16]); iot = sb("iot", [128, 16])
        cf = sb("cf", [128, 8, TH])
        sq = sb("sq", [128, TH]); mean = sb("mean", [128, TH]); rs = sb("rs", [128, TH]); tmpf = sb("tmpf", [128, TH])
        gsig = [sb(f"gsig{i}", [128, TH]) for i in range(2)]
        acc = sb("acc", [128, TH])

        S.dma("sp", "cload", [
            lambda e: e.dma_start(out=idf[:], in_=ident_in[:, :]),
            lambda e: e.dma_start(out=gm[:], in_=g_mix[:, :]),
            lambda e: e.dma_start(out=cv[:], in_=cvec[:, :]),
            lambda e: e.dma_start(out=ccv[:], in_=cconv[:, :]),
            lambda e: e.dma_start(out=pwst[:], in_=poolw.rearrange("a p d -> p a d")),
        ], writes=["idf", "gm", "cv", "ccv", "pwst"])
        dve("tensor_copy", idb[:], idf[:], r=["idf"], w=["idb"])
        dve("tensor_copy", pwb[:], pwst[:], r=["pwst"], w=["pwb"])
        S.v("pool", "memset", onesf[:], 1.0, writes=["onesf"])
        S.v("pool", "memset", epsT[:], EPS, writes=["epsT"])
        S.v("pool", "memset", lnepsT[:], LN_EPS, writes=["lnepsT"])
        dve("tensor_copy", iot[:], cv[:, 40:56], r=["cv"], w=["iot"])
        for gi, win in enumerate((2, 4, 8, 16)):
            dve("tensor_scalar", invc[:, gi, :], iot[:], cv[:, 32:33], float(win), ALU.add, ALU.min, r=["iot", "cv"], w=["invc"])
        dve("reciprocal", invc[:], invc[:], r=["invc"], w=["invc"])
        dve("memset", s1[:], 0.0, w=["s1"])
        dve("memset", s2[:], 0.0, w=["s2"])

        wctr = [0]

        def load_w(src_ap, nelem):
            i = wctr[0] % 2
            wctr[0] += 1
            S.dma("sp", f"w{i}", [lambda e: e.dma_start(out=wst[i][:, 0:nelem], in_=src_ap)], writes=[f"wst{i}"])
            cx.copy(wbf[i][:, 0:nelem], wst[i][:, 0:nelem], [f"wst{i}"], [f"wbf{i}"])
            return wbf[i], f"wbf{i}"

        for th in range(NTH):
            t0 = th * TH
            for s in range(TH // 128):
                r0 = t0 + s * 128
                norm_transpose(cx, x[r0:r0 + 128, :], 128, xt[s % 2], f"xt{s % 2}", xs, junk, ss, rstd, epsT, idb, gm, xnT, s * 128)
            S.dma("sp", "brl", [
                lambda e, t0=t0: e.dma_start(out=br[1][:], in_=ogdn[:, t0:t0 + TH].rearrange("(kc p) t -> p kc t", p=128)),
                lambda e, t0=t0: e.dma_start(out=br[3][:], in_=omla[:, t0:t0 + TH].rearrange("(kc p) t -> p kc t", p=128)),
            ], writes=["br1", "br3"])
            for c in range(8):
                gi = c // 2
                u = ub[c % 2]
                uk = f"ub{c % 2}"
                S.dma("sp", uk, [lambda e, u=u, c=c, t0=t0: e.dma_start(out=u[:], in_=upool[c * 128:(c + 1) * 128, t0:t0 + W])], writes=[uk])
                cur, ck = u, uk
                sh = 1
                bufs = [(s1, "s1"), (s2, "s2")]
                for k in range(gi + 1):
                    nxt, nk = bufs[k % 2]
                    dve("tensor_tensor", nxt[:, sh:W], cur[:, sh:W], cur[:, 0:W - sh], ALU.add, r=[ck], w=[nk])
                    cur, ck = nxt, nk
                    sh *= 2
                win = 2 ** (gi + 1)
                dve("scalar_tensor_tensor", dT[:, c, :], cur[:, HALO:W], 1.0 / win, u[:, HALO:W], ALU.mult, ALU.subtract, r=[ck, uk], w=["dT"])
                if th == 0:
                    dve("tensor_tensor", tmpf[:, 0:16], cur[:, HALO:HALO + 16], invc[:, gi, :], ALU.mult, r=[ck, "invc"], w=["tmpf"])
                    dve("tensor_tensor", dT[:, c, 0:16], tmpf[:, 0:16], u[:, HALO:HALO + 16], ALU.subtract, r=["tmpf", uk], w=["dT"])
            for g in range(4):
                for dj in range(2):
                    pt, pk = cx.pb()
                    for kc in range(2):
                        mm(pt[:, 0:TH], pwb[:, g * 2 + kc, dj * 128:(dj + 1) * 128], dT[:, g * 2 + kc, :], start=(kc == 0), stop=(kc == 1),
                           r=["pwb", "dT"], w=[pk])
                    act(br[0][:, g * 2 + dj, :], pt[:, 0:TH], AF.Copy, scale=cv[:, g * 2 + dj:g * 2 + dj + 1], r=[pk, "cv"], w=["br0"])
            for c in range(8):
                h = ub[c % 2]
                hk = f"ub{c % 2}"
                S.dma("sp", hk, [lambda e, h=h, c=c, t0=t0: e.dma_start(out=h[:], in_=hglu[c * 128:(c + 1) * 128, t0:t0 + W])], writes=[hk])
                dve("tensor_scalar", cf[:, c, :], h[:, 2:2 + TH], ccv[:, c * 31:c * 31 + 1], cv[:, 8 + c:9 + c], ALU.mult, ALU.add,
                    r=[hk, "ccv", "cv"], w=["cf"])
                for j in range(1, 31):
                    dve("scalar_tensor_tensor", cf[:, c, :], h[:, 2 + j:2 + j + TH], ccv[:, c * 31 + j:c * 31 + j + 1], cf[:, c, :], ALU.mult, ALU.add,
                        r=[hk, "ccv", "cf"], w=["cf"])
            pm, pmk = cx.pb()
            for c in range(8):
                mm(pm[:, 0:TH], onesf[:], cf[:, c, :], start=(c == 0), stop=(c == 7), r=["onesf", "cf"], w=[pmk])
            act(mean[:], pm[:, 0:TH], AF.Copy, scale=1.0 / 1024, r=[pmk], w=["mean"])
            pv, pvk = cx.pb()
            for c in range(8):
                dve("tensor_tensor", cf[:, c, :], cf[:, c, :], mean[:], ALU.subtract, r=["cf", "mean"], w=["cf"])
                act(sq[:], cf[:, c, :], AF.Square, r=["cf"], w=["sq"])
                mm(pv[:, 0:TH], onesf[:], sq[:], start=(c == 0), stop=(c == 7), r=["onesf", "sq"], w=[pvk])
            act(rs[:], pv[:, 0:TH], AF.Sqrt, scale=1.0 / 1024, bias=lnepsT[:, 0:1], r=[pvk, "lnepsT"], w=["rs"])
            dve("reciprocal", rs[:], rs[:], r=["rs"], w=["rs"])
            for c in range(8):
                dve("tensor_tensor", tmpf[:], cf[:, c, :], rs[:], ALU.mult, r=["cf", "rs"], w=["tmpf"])
                act(br[2][:, c, :], tmpf[:], AF.Silu, scale=cv[:, 16 + c:17 + c], bias=cv[:, 24 + c:25 + c], r=["tmpf", "cv"], w=["br2"])
            for f in range(16):
                for i in range(4):
                    blk = f * 4 + i
                    wgb, wgk = load_w(wg[blk, :, :], KC * 128)
                    pg, pgk = cx.pb()
                    for kc in range(KC):
                        mm(pg[:, 0:TH], wgb[:, kc * 128:(kc + 1) * 128], xnT[:, kc, :], start=(kc == 0), stop=(kc == KC - 1), r=[wgk, "xnT"], w=[pgk])
                    gs = gsig[i % 2]
                    act(gs[:], pg[:, 0:TH], AF.Sigmoid, r=[pgk], w=[f"gsig{i % 2}"])
                    wob, wok = load_w(wo[blk, :, :], 8 * 128)
                    py, pyk = cx.pb()
                    for kc in range(8):
                        mm(py[:, 0:TH], wob[:, kc * 128:(kc + 1) * 128], br[i][:, kc, :], start=(kc == 0), stop=(kc == 7), r=[wok, f"br{i}"], w=[pyk])
                    if i == 0:
                        dve("tensor_tensor", acc[:], py[:, 0:TH], gs[:], ALU.mult, r=[pyk, f"gsig{i % 2}"], w=["acc"])
                    else:
                        dve("tensor_tensor", tmpf[:], py[:, 0:TH], gs[:], ALU.mult, r=[pyk, f"gsig{i % 2}"], w=["tmpf"])
                        if i < 3:
                            dve("tensor_tensor", acc[:], acc[:], tmpf[:], ALU.add, r=["acc", "tmpf"], w=["acc"])
                        else:
                            dve("tensor_tensor", merged[:, f, :], acc[:], tmpf[:], ALU.add, r=["acc", "tmpf"], w=["merged"])
            for s in range(TH // 128):
                r0 = t0 + s * 128
                xb, xk = xt[s % 2], f"xt{s % 2}"
                S.dma("sp", xk, [lambda e, xb=xb, r0=r0: e.dma_start(out=xb[:], in_=x[r0:r0 + 128, :])], writes=[xk])
                for n in range(16):
                    wb_, wk_ = load_w(wout[n, :, :], KC * 128)
                    po, pok = cx.pb()
                    for kc in range(KC):
                        mm(po[:, 0:128], merged[:, kc, s * 128:(s + 1) * 128], wb_[:, kc * 128:(kc + 1) * 128], start=(kc == 0), stop=(kc == KC - 1),
                           r=["merged", wk_], w=[pok])
                    dve("tensor_tensor", xb[:, n * 128:(n + 1) * 128], xb[:, n * 128:(n + 1) * 128], po[:, 0:128], ALU.add, r=[xk, pok], w=[xk])
                S.dma("sp", f"xo{s % 2}", [lambda e, xb=xb, r0=r0: e.dma_start(out=xmid[r0:r0 + 128, :], in_=xb[:])], reads=[xk], writes=["xmid"])
        S.op("sp", None, reads=["xmid"])
        S.emit()
    return nc


def build_C2(T, final):
    TH = min(512, T)
    NTH = T // TH
    nc = bass.Bass("TRN2", target_bir_lowering=False)

    def din(name, shape, dt=F32):
        return nc.dram_tensor(name, shape, dt, kind="ExternalInput").ap()

    xm = din("xm", [2 + T, D])
    ident_in = din("ident", [128, 128])
    g_ffn = din("g_ffn", [128, KC])
    fcv = din("fcv", [128, 88 * 4])
    wup = din("wup", [88, 128, KC * 128])
    wdn = din("wdn", [32, 128, 22 * 128])
    fng = din("fng", [1, D])
    xout = nc.dram_tensor("xout", [T, D], F32, kind="ExternalOutput").ap()

    with contextlib.ExitStack() as st:
        cx = Ctx(nc, st)
        S = cx.S
        sb, dve, act, mm = cx.sb, cx.dve, cx.act, cx.mm
        hnT = sb("xnT", [128, KC, TH], BF16)
        hhT = sb("hhT", [128, KC, 2], BF16)
        actt = sb("actt", [128, NFC, TH], BF16)
        wst = [sb(f"wst{i}", [128, 22 * 128]) for i in range(2)]
        wbf = [sb(f"wbf{i}", [128, 22 * 128], BF16) for i in range(2)]
        xt = [sb(f"xt{i}", [128, D]) for i in range(2)]
        xs = sb("xs", [128, D], BF16); junk = sb("junk", [128, D], BF16)
        ss = sb("ss", [128, 1]); rstd = sb("rstd", [128, 1]); epsT = sb("epsT", [128, 1])
        idf = sb("idf", [128, 128]); idb = sb("idb", [128, 128], BF16)
        gm = sb("gm", [128, KC]); fc = sb("fc", [128, 88 * 4])
        hprev = sb("hprev", [128, 88, 2])
        hext = [sb(f"hext{i}", [128, 2 + TH]) for i in range(2)]
        yg = sb("yg", [128, TH]); yu = sb("yu", [128, TH])
        yfm = sb("yfm", [128, KC, TH])
        fg = sb("fg", [128, D])

        S.dma("sp", "cload", [
            lambda e: e.dma_start(out=idf[:], in_=ident_in[:, :]),
            lambda e: e.dma_start(out=gm[:], in_=g_ffn[:, :]),
            lambda e: e.dma_start(out=fc[:], in_=fcv[:, :]),
            lambda e: e.dma_start(out=fg[:], in_=fng[0:1, :].broadcast_to([128, D])),
        ], writes=["idf", "gm", "fc", "fg"])
        dve("tensor_copy", idb[:], idf[:], r=["idf"], w=["idb"])
        S.v("pool", "memset", epsT[:], EPS, writes=["epsT"])

        wctr = [0]

        def load_w(src_ap, nelem):
            i = wctr[0] % 2
            wctr[0] += 1
            S.dma("sp", f"w{i}", [lambda e: e.dma_start(out=wst[i][:, 0:nelem], in_=src_ap)], writes=[f"wst{i}"])
            cx.copy(wbf[i][:, 0:nelem], wst[i][:, 0:nelem], [f"wst{i}"], [f"wbf{i}"])
            return wbf[i], f"wbf{i}"

        norm_transpose(cx, xm[0:2, :], 2, xt[0], "xt0", xs, junk, ss, rstd, epsT, idb, gm, hhT, 0)

        for th in range(NTH):
            t0 = th * TH
            for s in range(TH // 128):
                r0 = 2 + t0 + s * 128
                norm_transpose(cx, xm[r0:r0 + 128, :], 128, xt[s % 2], f"xt{s % 2}", xs, junk, ss, rstd, epsT, idb, gm, hnT, s * 128)
            for c in range(NFC):
                for half in range(2):
                    ch = c * 2 + half
                    wb_, wk_ = load_w(wup[ch, :, :], KC * 128)
                    he, hk = hext[half], f"hext{half}"
                    if th == 0:
                        ph, phk = cx.pb()
                        for kc in range(KC):
                            mm(ph[:, 0:2], wb_[:, kc * 128:(kc + 1) * 128], hhT[:, kc, :], start=(kc == 0), stop=(kc == KC - 1), r=[wk_, "xnT"], w=[phk])
                        dve("tensor_copy", he[:, 0:2], ph[:, 0:2], r=[phk], w=[hk])
                    else:
                        dve("tensor_copy", he[:, 0:2], hprev[:, ch, :], r=["hprev"], w=[hk])
                    pu, puk = cx.pb()
                    for kc in range(KC):
                        mm(pu[:, 0:TH], wb_[:, kc * 128:(kc + 1) * 128], hnT[:, kc, :], start=(kc == 0), stop=(kc == KC - 1), r=[wk_, "xnT"], w=[puk])
                    act(he[:, 2:2 + TH], pu[:, 0:TH], AF.Copy, r=[puk], w=[hk])
                    dve("tensor_copy", hprev[:, ch, :], he[:, TH:TH + 2], r=[hk], w=["hprev"])
                    y, yk = (yg, "yg") if half == 0 else (yu, "yu")
                    dve("tensor_scalar", y[:], he[:, 2:2 + TH], fc[:, ch * 4 + 2:ch * 4 + 3], fc[:, ch * 4 + 3:ch * 4 + 4], ALU.mult, ALU.add,
                        r=[hk, "fc"], w=[yk])
                    dve("scalar_tensor_tensor", y[:], he[:, 1:1 + TH], fc[:, ch * 4 + 1:ch * 4 + 2], y[:], ALU.mult, ALU.add, r=[hk, "fc", yk], w=[yk])
                    dve("scalar_tensor_tensor", y[:], he[:, 0:TH], fc[:, ch * 4:ch * 4 + 1], y[:], ALU.mult, ALU.add, r=[hk, "fc", yk], w=[yk])
                act(yg[:], yg[:], AF.Silu, r=["yg"], w=["yg"])
                dve("tensor_tensor", actt[:, c, :], yg[:], yu[:], ALU.mult, r=["yg", "yu"], w=["actt"])
            for f in range(16):
                pd, pdk = cx.pb()
                for half in range(2):
                    wb_, wk_ = load_w(wdn[f * 2 + half, :, :], 22 * 128)
                    for k2 in range(22):
                        kc = half * 22 + k2
                        mm(pd[:, 0:TH], wb_[:, k2 * 128:(k2 + 1) * 128], actt[:, kc, :], start=(kc == 0), stop=(kc == NFC - 1), r=[wk_, "actt"], w=[pdk])
                cx.copy(yfm[:, f, :], pd[:, 0:TH], [pdk], ["yfm"])
            for s in range(TH // 128):
                r0 = t0 + s * 128
                xb, xk = xt[s % 2], f"xt{s % 2}"
                S.dma("sp", xk, [lambda e, xb=xb, r0=r0: e.dma_start(out=xb[:], in_=xm[2 + r0:2 + r0 + 128, :])], writes=[xk])
                for n in range(4):
                    pt, pk = cx.pb()
                    for j in range(4):
                        f = n * 4 + j
                        S.tr(pt[:, j * 128:(j + 1) * 128], yfm[:, f, s * 128:(s + 1) * 128], idf[:], reads=["yfm", "idf"], writes=[pk])
                    dve("tensor_tensor", xb[:, n * 512:(n + 1) * 512], xb[:, n * 512:(n + 1) * 512], pt[:, 0:512], ALU.add, r=[xk, pk], w=[xk])
                if final:
                    dve("memset", ss[:, 0:1], 0.0, w=["ss"])
                    act(junk[:], xb[:], AF.Square, accum_out=ss[:, 0:1], r=[xk, "ss"], w=["junk", "ss"])
                    act(rstd[:, 0:1], ss[:, 0:1], AF.Sqrt, scale=1.0 / D, bias=epsT[:, 0:1], r=["ss", "epsT"], w=["rstd"])
                    dve("reciprocal", rstd[:, 0:1], rstd[:, 0:1], r=["rstd"], w=["rstd"])
                    dve("scalar_tensor_tensor", xb[:], xb[:], rstd[:, 0:1], fg[:], ALU.mult, ALU.mult, r=[xk, "rstd", "fg"], w=[xk])
                S.dma("sp", f"xo{s % 2}", [lambda e, xb=xb, r0=r0: e.dma_start(out=xout[r0:r0 + 128, :], in_=xb[:])], reads=[xk], writes=["xout"])
        S.op("sp", None, reads=["xout"])
        S.emit()
    return nc

import numpy as np
import ml_dtypes

NCORES = 8
_BF = ml_dtypes.bfloat16
_PROGS = {}


def _prog(key, fn):
    if key not in _PROGS:
        _PROGS[key] = fn()
    return _PROGS[key]


def _cm(v, n):
    return np.ascontiguousarray(np.asarray(v, np.float32).reshape(n, 128).T)


def _run(nc, in_maps):
    res = run_bass_kernel_spmd(nc, in_maps, core_ids=list(range(len(in_maps))))
    return res.results


def kernel(x, positions, mix_norm, w_in, pool_w, pool_scale, gdn_conv_w, gdn_a_log, gdn_dt_bias, gdn_norm,
           conf_conv_w, conf_conv_b, conf_ln_g, conf_ln_b, mla_q_norm, mla_w_uq, mla_kv_norm, mla_w_ukv,
           w_pool_out, w_gdn_out, w_conf_out, w_mla_out, w_out, ffn_norm, ffn_w_up, ffn_conv_w, ffn_conv_b,
           ffn_w_down, final_norm):
    x = np.asarray(x, np.float32)
    B, S, D = x.shape
    NT = B * S
    T = NT // NCORES
    CPS = S // T
    depth = w_in.shape[0]
    pos_flat = np.ascontiguousarray(np.asarray(positions, np.int32).reshape(1, NT))
    xcur = x.reshape(NT, D)

    invf = (10000.0 ** (-np.arange(0, 64, 2, dtype=np.float32) / 64)).astype(np.float32)
    consts = np.zeros((128, 8), np.float32)
    consts[0:32, 0] = invf
    consts[32:64, 0] = invf
    consts[0:32, 1] = -1.0
    consts[32:64, 1] = 1.0
    ident = np.eye(128, dtype=np.float32)
    ii = np.arange(128)
    masks = np.zeros((128, 640), np.float32)
    masks[:, 0:128] = np.eye(128)
    masks[:, 128:256] = (ii[:, None] <= ii[None, :])
    masks[:, 256:384] = (ii[:, None] > ii[None, :])
    masks[:, 384:512] = (ii[None, :] >= ii[:, None])
    masks[:, 512:640] = (ii[None, :] > ii[:, None])

    ncA = _prog(("A", T), lambda: build_A(T))
    ncB = _prog(("B", S, B), lambda: build_B(S, B))
    ncC1 = _prog(("C1", T), lambda: build_C1(T))

    for l in range(depth):
        wl = np.asarray(w_in[l], np.float32)
        kr = wl[:, 7184:7248]
        commonA = dict(
            consts=consts, ident=ident, g_mix=_cm(mix_norm[l], 16), qn=_cm(mla_q_norm[l], 4), kvn=_cm(mla_kv_norm[l], 4),
            w_pool=np.ascontiguousarray(wl[:, 0:1024]), w_qkvz=np.ascontiguousarray(wl[:, 1024:4096]),
            w_ab=np.ascontiguousarray(wl[:, 4096:4112]), w_conf=np.ascontiguousarray(wl[:, 4112:6160]),
            w_cq=np.ascontiguousarray(wl[:, 6160:6672]), w_ckv=np.ascontiguousarray(wl[:, 6672:7184]),
            w_kr=np.ascontiguousarray(np.concatenate([kr, kr[:, 32:], kr[:, :32]], axis=1)))
        insA = []
        for c in range(NCORES):
            d = dict(commonA)
            d["x"] = np.ascontiguousarray(xcur[c * T:(c + 1) * T])
            d["pos"] = np.ascontiguousarray(pos_flat[:, c * T:(c + 1) * T])
            insA.append(d)
        rA = _run(ncA, insA)
        cat = lambda k: np.concatenate([r[k] for r in rA], axis=1)
        G_pool = cat("o_pool")
        G_qkvz = cat("o_qkvz")
        G_hglu = cat("o_hglu")
        G_cq = cat("o_cq")
        G_ckv = cat("o_ckv")
        G_kpe = cat("o_kpe")
        G_ab = np.concatenate([r["o_ab"] for r in rA], axis=0)
        del rA, insA
        cwf = np.asarray(gdn_conv_w[l], np.float32)
        insB = []
        for h in range(NCORES):
            kh = h // 2
            gcw = np.concatenate([cwf[:, kh * 128:(kh + 1) * 128].T, cwf[:, 512 + kh * 128:512 + (kh + 1) * 128].T,
                                  cwf[:, 1024 + h * 128:1024 + (h + 1) * 128].T], axis=1)
            gsc = np.zeros((128, 4), np.float32)
            gsc[:, 0] = gdn_a_log[l][h]
            gsc[:, 1] = gdn_dt_bias[l][h]
            gsc[:, 2] = gdn_norm[l]
            wuq = np.asarray(mla_w_uq[l][:, h * 192:(h + 1) * 192], np.float32)
            wq = np.concatenate([wuq[:, :128], wuq[:, 128:192], wuq[:, 160:192], wuq[:, 128:160]], axis=1)
            wkv = np.asarray(mla_w_ukv[l][:, h * 256:(h + 1) * 256], np.float32)
            insB.append(dict(
                gq=np.ascontiguousarray(G_qkvz[kh * 128:(kh + 1) * 128]),
                gk=np.ascontiguousarray(G_qkvz[512 + kh * 128:512 + (kh + 1) * 128]),
                gv=np.ascontiguousarray(G_qkvz[1024 + h * 128:1024 + (h + 1) * 128]),
                gz=np.ascontiguousarray(G_qkvz[2048 + h * 128:2048 + (h + 1) * 128]),
                gab=np.ascontiguousarray(np.stack([G_ab[:, h], G_ab[:, 8 + h]], axis=-1)),
                gcw=np.ascontiguousarray(gcw), gsc=gsc, masks=masks,
                cqn=G_cq, ckvn=G_ckv, kpe=G_kpe,
                wq=np.ascontiguousarray(wq), wkv=np.ascontiguousarray(wkv), pos=pos_flat, consts=consts))
        rB = _run(ncB, insB)
        OG = np.concatenate([r["o_gdn"] for r in rB], axis=0)
        OM = np.concatenate([r["o_mla"] for r in rB], axis=0)
        del rB, insB, G_qkvz, G_cq, G_ckv, G_kpe, G_ab
        Wg = np.ascontiguousarray(wl[:, 7248:].reshape(16, 128, 4, 16, 128).transpose(3, 2, 1, 0, 4)).reshape(64, 128, 2048)
        Wo4 = np.stack([np.asarray(w, np.float32) for w in (w_pool_out[l], w_gdn_out[l], w_conf_out[l], w_mla_out[l])], axis=0)
        Wo = np.ascontiguousarray(Wo4.reshape(4, 8, 128, 16, 128).transpose(3, 0, 2, 1, 4)).reshape(64, 128, 1024)
        Wout = np.ascontiguousarray(np.asarray(w_out[l], np.float32).reshape(16, 128, 16, 128).transpose(2, 1, 0, 3)).reshape(16, 128, 2048)
        poolw = np.ascontiguousarray(np.asarray(pool_w[l], np.float32).reshape(8, 128, 256))
        cconv = np.ascontiguousarray(np.asarray(conf_conv_w[l], np.float32).reshape(31, 8, 128).transpose(2, 1, 0)).reshape(128, 248)
        cvec0 = np.zeros((128, 64), np.float32)
        cvec0[:, 0:8] = _cm(pool_scale[l], 8)
        cvec0[:, 8:16] = _cm(conf_conv_b[l], 8)
        cvec0[:, 16:24] = _cm(conf_ln_g[l], 8)
        cvec0[:, 24:32] = _cm(conf_ln_b[l], 8)
        cvec0[:, 40:56] = np.arange(16, dtype=np.float32)[None, :]
        gmix = _cm(mix_norm[l], 16)
        insC = []
        for c in range(NCORES):
            q = c % CPS
            g0 = c * T
            cv = cvec0.copy()
            cv[:, 32] = q * T + 1
            up = np.zeros((1024, 32 + T), np.float32)
            hg = np.zeros((1024, 32 + T), np.float32)
            up[:, 32:] = G_pool[:, g0:g0 + T]
            hg[:, 32:] = G_hglu[:, g0:g0 + T]
            if q > 0:
                up[:, :32] = G_pool[:, g0 - 32:g0]
                hg[:, :32] = G_hglu[:, g0 - 32:g0]
            insC.append(dict(
                x=np.ascontiguousarray(xcur[g0:g0 + T]), upool=up, hglu=hg,
                ogdn=np.ascontiguousarray(OG[:, g0:g0 + T]), omla=np.ascontiguousarray(OM[:, g0:g0 + T]),
                ident=ident, g_mix=gmix, cvec=cv, cconv=cconv, poolw=poolw, wg=Wg, wo=Wo, wout=Wout))
        rC = _run(ncC1, insC)
        xmid = np.concatenate([r["xmid"] for r in rC], axis=0)
        del rC, insC, Wg, Wo, Wout, G_pool, G_hglu, OG, OM
        final = (l == depth - 1)
        ncC2 = _prog(("C2", T, final), lambda: build_C2(T, final))
        Wup = np.ascontiguousarray(np.asarray(ffn_w_up[l], np.float32).reshape(16, 128, 2, 44, 128).transpose(3, 2, 1, 0, 4)).reshape(88, 128, 2048)
        Wdn = np.ascontiguousarray(np.asarray(ffn_w_down[l], np.float32).reshape(2, 22, 128, 16, 128).transpose(3, 0, 2, 1, 4)).reshape(32, 128, 2816)
        fcw = np.asarray(ffn_conv_w[l], np.float32).reshape(3, 2, 44, 128).transpose(3, 2, 1, 0)
        fcb = np.asarray(ffn_conv_b[l], np.float32).reshape(2, 44, 128).transpose(2, 1, 0)[..., None]
        fcv = np.ascontiguousarray(np.concatenate([fcw, fcb], axis=-1)).reshape(128, 352)
        gffn = _cm(ffn_norm[l], 16)
        fng = np.ascontiguousarray(np.asarray(final_norm, np.float32).reshape(1, D))
        insD = []
        for c in range(NCORES):
            q = c % CPS
            g0 = c * T
            xm = np.zeros((2 + T, D), np.float32)
            xm[2:] = xmid[g0:g0 + T]
            if q > 0:
                xm[:2] = xmid[g0 - 2:g0]
            insD.append(dict(xm=xm, ident=ident, g_ffn=gffn, fcv=fcv, wup=Wup, wdn=Wdn, fng=fng))
        rD = _run(ncC2, insD)
        xcur = np.concatenate([r["xout"] for r in rD], axis=0)
        del rD, insD, Wup, Wdn
    return np.ascontiguousarray(xcur.reshape(B, S, D).astype(np.float32))
```

```python
import contextlib
import numpy as np
import concourse.bass as bass
import concourse.mybir as mybir
from concourse.bass_utils import run_bass_kernel_spmd

F32 = mybir.dt.float32
BF16 = mybir.dt.bfloat16
I32 = mybir.dt.int32
AF = mybir.ActivationFunctionType
ALU = mybir.AluOpType
AX = mybir.AxisListType


class Op:
    __slots__ = ("eng", "fn", "fns", "deps", "signal", "sigval", "chan", "doneval")

    def __init__(self, eng):
        self.eng = eng
        self.fn = None
        self.fns = None
        self.deps = ()
        self.signal = False
        self.sigval = 0
        self.chan = None
        self.doneval = 0


class Sched:
    def __init__(self, nc, same_eng_sync=True):
        self.nc = nc
        self.ops = []
        self.last_w = {}
        self.readers = {}
        self.chan_last = {}
        self.chan_cnt = {}
        self.same = same_eng_sync

    def _deps(self, reads, writes):
        d = set()
        for k in reads:
            w = self.last_w.get(k)
            if w is not None:
                d.add(w)
        for k in writes:
            w = self.last_w.get(k)
            if w is not None:
                d.add(w)
            r = self.readers.get(k)
            if r:
                d.update(r.values())
        return d

    def _commit(self, o, reads, writes):
        for k in reads:
            r = self.readers.setdefault(k, {})
            if o.chan is None:
                r[o.eng] = o
            else:
                r[("dma", o.chan)] = o
        for k in writes:
            self.last_w[k] = o
            self.readers[k] = {}

    def op(self, eng, fn, reads=(), writes=()):
        o = Op(eng)
        o.fn = fn
        o.deps = self._deps(reads, writes)
        self._commit(o, reads, writes)
        self.ops.append(o)
        return o

    def dma(self, queue, chan, fns, reads=(), writes=()):
        o = Op(queue)
        o.fns = list(fns)
        o.chan = chan
        d = self._deps(reads, writes)
        p = self.chan_last.get(chan)
        if p is not None:
            d.add(p)
        o.deps = d
        self.chan_last[chan] = o
        self.chan_cnt[chan] = self.chan_cnt.get(chan, 0) + len(o.fns)
        o.doneval = 16 * self.chan_cnt[chan]
        self._commit(o, reads, writes)
        self.ops.append(o)
        return o

    def mm(self, out, lhsT, rhs, start=True, stop=True, reads=(), writes=(), **kw):
        return self.op("pe", lambda e: e.matmul(out, lhsT, rhs, start=start, stop=stop, **kw), reads, writes)

    def tr(self, out, in_, ident, reads=(), writes=()):
        return self.op("pe", lambda e: e.transpose(out, in_, ident), reads, writes)

    def act(self, out, in_, func, reads=(), writes=(), **kw):
        return self.op("act", lambda e: e.activation(out, in_, func, **kw), reads, writes)

    def v(self, eng, name, *args, reads=(), writes=(), **kw):
        return self.op(eng, lambda e: getattr(e, name)(*args, **kw), reads, writes)

    def emit(self):
        nc = self.nc
        same = self.same
        for o in self.ops:
            for p in o.deps:
                if p.chan is None:
                    if p.eng == o.eng and (p.eng == "pe" or not same):
                        continue
                    p.signal = True
        cnt = {}
        for o in self.ops:
            if o.chan is None and o.signal:
                cnt[o.eng] = cnt.get(o.eng, 0) + 1
                o.sigval = cnt[o.eng]
        engs = ["pe", "act", "dve", "pool", "sp"]
        by_eng = {e: [] for e in engs}
        for o in self.ops:
            by_eng[o.eng].append(o)
        with contextlib.ExitStack() as st:
            engsem = {e: st.enter_context(nc.semaphore("s_" + e)) for e in engs}
            chansem = {c: st.enter_context(nc.semaphore("c_" + c)) for c in self.chan_cnt}
            block = st.enter_context(nc.Block())

            def run(ename, eng):
                seen = {}
                for o in by_eng[ename]:
                    waits = {}
                    for p in o.deps:
                        if p.chan is not None:
                            key, sem, val = "c_" + p.chan, chansem[p.chan], p.doneval
                        else:
                            if p.eng == ename and (ename == "pe" or not same):
                                continue
                            key, sem, val = "s_" + p.eng, engsem[p.eng], p.sigval
                        if val > waits.get(key, (None, 0))[1]:
                            waits[key] = (sem, val)
                    for key, (sem, val) in waits.items():
                        if seen.get(key, 0) < val:
                            eng.wait_ge(sem, val)
                            seen[key] = val
                    if o.chan is None:
                        if o.fn is not None:
                            ins = o.fn(eng)
                            if o.signal:
                                ins.then_inc(engsem[ename], 1)
                        elif o.signal:
                            raise RuntimeError("wait-only op cannot signal")
                    else:
                        for f in o.fns:
                            f(eng).then_inc(chansem[o.chan], 16)

            @block.tensor
            def _(e):
                run("pe", e)

            @block.scalar
            def _(e):
                run("act", e)

            @block.vector
            def _(e):
                run("dve", e)

            @block.gpsimd
            def _(e):
                run("pool", e)

            @block.sync
            def _(e):
                run("sp", e)
        return cnt

import contextlib, math
import numpy as np

D = 2048
KC = 16
N_POOL = 1024
N_QKVZ = 3072
N_CONF = 2048
EPS = 1e-6
PI = math.pi


def build_A(T):
    assert T % 512 == 0 or T in (128, 256)
    TT = min(512, T)
    NT = T // TT
    NS = T // 128
    nc = bass.Bass("TRN2", target_bir_lowering=False)

    def din(name, shape, dt=F32):
        return nc.dram_tensor(name, shape, dt, kind="ExternalInput").ap()

    def dout(name, shape, dt=F32):
        return nc.dram_tensor(name, shape, dt, kind="ExternalOutput").ap()

    x = din("x", [T, D])
    pos = din("pos", [1, T], I32)
    consts = din("consts", [128, 8])
    ident_in = din("ident", [128, 128])
    g_mix = din("g_mix", [128, KC])
    qn = din("qn", [128, 4])
    kvn = din("kvn", [128, 4])
    w_pool = din("w_pool", [D, N_POOL])
    w_qkvz = din("w_qkvz", [D, N_QKVZ])
    w_ab = din("w_ab", [D, 16])
    w_conf = din("w_conf", [D, N_CONF])
    w_cq = din("w_cq", [D, 512])
    w_ckv = din("w_ckv", [D, 512])
    w_kr = din("w_kr", [D, 128])
    o_pool = dout("o_pool", [N_POOL, T])
    o_qkvz = dout("o_qkvz", [N_QKVZ, T])
    o_ab = dout("o_ab", [T, 16])
    o_hglu = dout("o_hglu", [1024, T])
    o_cq = dout("o_cq", [512, T], BF16)
    o_ckv = dout("o_ckv", [512, T], BF16)
    o_kpe = dout("o_kpe", [64, T], BF16)

    S = Sched(nc)
    with contextlib.ExitStack() as st:
        def sb(name, shape, dt=F32):
            return st.enter_context(nc.sbuf_tensor(name, shape, dt))

        def ps(name, shape, dt=F32):
            return st.enter_context(nc.psum_tensor(name, shape, dt))

        xnT = sb("xnT", [128, KC, T], BF16)
        wst = [sb("wst0", [128, KC, 256])]
        wbf = [sb(f"wbf{i}", [128, KC, 256], BF16) for i in range(2)]
        xt = [sb(f"xt{i}", [128, D]) for i in range(2)]
        xs = sb("xs", [128, D], BF16)
        junk = sb("junk", [128, D], BF16)
        ss = sb("ss", [128, NS])
        rstd = sb("rstd", [128, NS])
        cst = sb("cst", [128, 8])
        gm = sb("gm", [128, KC])
        qn_t = sb("qn_t", [128, 4])
        kvn_t = sb("kvn_t", [128, 4])
        idf = sb("idf", [128, 128])
        idb = sb("idb", [128, 128], BF16)
        onesf = sb("onesf", [128, 128])
        epsT = sb("epsT", [128, 1])
        ost = [sb(f"ost{i}", [128, T]) for i in range(3)]
        obf = [sb(f"obf{i}", [128, T], BF16) for i in range(2)]
        stash = sb("stash", [128, 4, T])
        sq = sb("sq", [128, TT])
        rb = sb("rb", [128, T])
        abw = sb("abw", [128, KC, 16], BF16)
        absb = sb("absb", [128, NS, 16])
        posi = stash[0:64, 0, :].bitcast(I32)
        ang = rb[0:64, :]
        cosT = stash[0:64, 1, :]
        sinT = stash[0:64, 2, :]
        P = [ps(f"P{i}", [128, 512]) for i in range(8)]
        rr = ost[2][0:64, :]
        kf = ost[0][0:64, :]
        msk = ost[1][0:64, :]
        ki = posi

        S.dma("sp", "cload", [
            lambda e: e.dma_start(out=cst[:], in_=consts[:, :]),
            lambda e: e.dma_start(out=gm[:], in_=g_mix[:, :]),
            lambda e: e.dma_start(out=qn_t[:], in_=qn[:, :]),
            lambda e: e.dma_start(out=kvn_t[:], in_=kvn[:, :]),
            lambda e: e.dma_start(out=idf[:], in_=ident_in[:, :]),
        ], writes=["cst", "gm", "qn_t", "kvn_t", "idf"])
        S.v("dve", "tensor_copy", idb[:], idf[:], reads=["idf"], writes=["idb"])
        S.v("pool", "memset", onesf[:], 1.0, writes=["onesf"])
        S.v("pool", "memset", epsT[:], EPS, writes=["epsT"])
        S.v("pool", "memset", ss[:], 0.0, writes=["ss"])

        for s in range(NS):
            xb = xt[s % 2]
            xk = f"xt{s % 2}"
            S.dma("sp", xk, [lambda e, xb=xb, s=s: e.dma_start(out=xb[:], in_=x[s * 128:(s + 1) * 128, :])], writes=[xk])
            S.act(junk[:], xb[:], AF.Square, accum_out=ss[:, s:s + 1], reads=[xk, "ss"], writes=["junk", "ss"])
            S.act(rstd[:, s:s + 1], ss[:, s:s + 1], AF.Sqrt, scale=1.0 / D, bias=epsT[:, 0:1], reads=["ss", "epsT"], writes=["rstd"])
            S.v("dve", "reciprocal", rstd[:, s:s + 1], rstd[:, s:s + 1], reads=["rstd"], writes=["rstd"])
            S.v("dve", "tensor_scalar", xs[:], xb[:], rstd[:, s:s + 1], None, ALU.mult, reads=[xk, "rstd"], writes=["xs"])
            for h in range(2):
                pt = P[(2 * s + h) % 8]
                pk = f"P{(2 * s + h) % 8}"
                pv = pt[:].bitcast(BF16)
                for j in range(8):
                    kc = h * 8 + j
                    S.tr(pv[:, j * 128:(j + 1) * 128], xs[:, kc * 128:(kc + 1) * 128], idb[:], reads=["xs", "idb"], writes=[pk])
                eng = "act" if h == 0 else "dve"
                dst = xnT[:, h * 8:(h + 1) * 8, s * 128:(s + 1) * 128]
                src = pv.rearrange("p (j t) -> p j t", j=8)
                if eng == "act":
                    S.act(dst, src, AF.Copy, reads=[pk], writes=["xnT"])
                else:
                    S.v("dve", "tensor_copy", dst, src, reads=[pk], writes=["xnT"])

        wctr = [0]

        def load_w(wdram, c0, ncols):
            i = wctr[0] % 2
            wctr[0] += 1
            S.dma("sp", "w0", [lambda e: e.dma_start(out=wst[0][:, :, 0:ncols],
                                                      in_=wdram[:, c0:c0 + ncols].rearrange("(kc p) c -> p kc c", p=128))],
                  writes=["wst0"])
            for kc in range(KC):
                if kc % 2 == 0:
                    S.act(wbf[i][:, kc, 0:ncols], wst[0][:, kc, 0:ncols], AF.Copy, scale=gm[:, kc:kc + 1],
                          reads=["wst0", "gm"], writes=[f"wbf{i}"])
                else:
                    S.v("dve", "tensor_scalar", wbf[i][:, kc, 0:ncols], wst[0][:, kc, 0:ncols], gm[:, kc:kc + 1], None, ALU.mult,
                        reads=["wst0", "gm"], writes=[f"wbf{i}"])
            return wbf[i], f"wbf{i}"

        pctr = [0]

        def project(wb, wk, coff, M):
            base = (pctr[0] % 2) * 4
            pctr[0] += 1
            outs = []
            for kc in range(KC):
                for tt in range(NT):
                    S.mm(P[base + tt][0:M, 0:TT], wb[:, kc, coff:coff + M], xnT[:, kc, tt * TT:(tt + 1) * TT],
                         start=(kc == 0), stop=(kc == KC - 1), reads=[wk, "xnT"], writes=[f"P{base + tt}"])
            for tt in range(NT):
                outs.append((P[base + tt], f"P{base + tt}"))
            return outs

        octr = [0]
        ectr = [0]

        def evac_copy(dst, src, reads, writes):
            ectr[0] += 1
            if ectr[0] % 2 == 0:
                S.act(dst, src, AF.Copy, reads=reads, writes=writes)
            else:
                S.v("dve", "tensor_copy", dst, src, reads=reads, writes=writes)

        def plain_chunks(wdram, ncols_total, odram):
            for c0 in range(0, ncols_total, 256):
                wb, wk = load_w(wdram, c0, 256)
                for j in range(2):
                    outs = project(wb, wk, j * 128, 128)
                    oi = octr[0] % 3
                    octr[0] += 1
                    for tt, (pt, pk) in enumerate(outs):
                        evac_copy(ost[oi][:, tt * TT:(tt + 1) * TT], pt[:, 0:TT], [pk], [f"ost{oi}"])
                    r0 = c0 + j * 128
                    S.dma("pool", f"o{oi}", [lambda e, oi=oi, r0=r0: e.dma_start(out=odram[r0:r0 + 128, :], in_=ost[oi][:])],
                          reads=[f"ost{oi}"], writes=[odram.name if hasattr(odram, "name") else "od"])

        plain_chunks(w_pool, N_POOL, o_pool)
        plain_chunks(w_qkvz, N_QKVZ, o_qkvz)

        for c in range(8):
            for half in range(2):
                wb, wk = load_w(w_conf, (c + 8 * half) * 128, 128)
                outs = project(wb, wk, 0, 128)
                if half == 0:
                    oi = octr[0] % 3
                    octr[0] += 1
                    for tt, (pt, pk) in enumerate(outs):
                        evac_copy(ost[oi][:, tt * TT:(tt + 1) * TT], pt[:, 0:TT], [pk], [f"ost{oi}"])
                else:
                    for tt, (pt, pk) in enumerate(outs):
                        S.act(rb[:, tt * TT:(tt + 1) * TT], pt[:, 0:TT], AF.Sigmoid, reads=[pk], writes=["rb"])
                    S.v("dve", "tensor_tensor", ost[oi][:], ost[oi][:], rb[:], ALU.mult, reads=[f"ost{oi}", "rb"], writes=[f"ost{oi}"])
                    S.dma("pool", f"o{oi}", [lambda e, oi=oi, c=c: e.dma_start(out=o_hglu[c * 128:(c + 1) * 128, :], in_=ost[oi][:])],
                          reads=[f"ost{oi}"], writes=["o_hglu"])

        for (wd, nt, od, oname) in ((w_cq, qn_t, o_cq, "o_cq"), (w_ckv, kvn_t, o_ckv, "o_ckv")):
            for c0 in range(0, 512, 256):
                wb, wk = load_w(wd, c0, 256)
                for j in range(2):
                    outs = project(wb, wk, j * 128, 128)
                    ci = c0 // 128 + j
                    for tt, (pt, pk) in enumerate(outs):
                        evac_copy(stash[:, ci, tt * TT:(tt + 1) * TT], pt[:, 0:TT], [pk], ["stash"])
            for tt in range(NT):
                pt, pk = P[tt], f"P{tt}"
                for ci in range(4):
                    S.act(sq[:], stash[:, ci, tt * TT:(tt + 1) * TT], AF.Square, reads=["stash"], writes=["sq"])
                    S.mm(pt[:, 0:TT], onesf[:], sq[:], start=(ci == 0), stop=(ci == 3), reads=["onesf", "sq"], writes=[pk])
                S.act(rb[:, tt * TT:(tt + 1) * TT], pt[:, 0:TT], AF.Sqrt, scale=1.0 / 512, bias=epsT[:, 0:1], reads=[pk, "epsT"], writes=["rb"])
            S.v("dve", "reciprocal", rb[:], rb[:], reads=["rb"], writes=["rb"])
            for ci in range(4):
                bi = ci % 2
                S.v("dve", "scalar_tensor_tensor", obf[bi][:], stash[:, ci, :], nt[:, ci:ci + 1], rb[:], ALU.mult, ALU.mult,
                    reads=["stash", "rb", "qn_t", "kvn_t"], writes=[f"obf{bi}"])
                S.dma("pool", f"ob{bi}", [lambda e, bi=bi, ci=ci, od=od: e.dma_start(out=od[ci * 128:(ci + 1) * 128, :], in_=obf[bi][:])],
                      reads=[f"obf{bi}"], writes=[oname])

        S.dma("sp", "pload", [lambda e: e.dma_start(out=posi[:], in_=pos[0:1, :].broadcast_to([64, T]))], writes=["stash"])
        S.v("dve", "tensor_copy", ang[:], posi[:], reads=["stash"], writes=["rb"])
        S.v("dve", "tensor_scalar", ang[:], ang[:], cst[0:64, 0:1], None, ALU.mult, reads=["rb", "cst"], writes=["rb"])

        def reduce_sin(dst, shift, scale_ap, tag):
            S.v("dve", "tensor_scalar", rr[:], ang[:], 1.0 / (2 * PI), (shift + PI) / (2 * PI), ALU.mult, ALU.add,
                reads=["rb"], writes=["ost2"])
            S.v("dve", "tensor_copy", ki[:], rr[:], reads=["ost2"], writes=["stash"])
            S.v("dve", "tensor_copy", kf[:], ki[:], reads=["stash"], writes=["ost0"])
            S.v("dve", "tensor_scalar", rr[:], ang[:], shift, None, ALU.add, reads=["rb"], writes=["ost2"])
            S.v("dve", "scalar_tensor_tensor", rr[:], kf[:], -2 * PI, rr[:], ALU.mult, ALU.add, reads=["ost0", "ost2"], writes=["ost2"])
            S.v("dve", "tensor_scalar", msk[:], rr[:], -PI, 2 * PI, ALU.is_lt, ALU.mult, reads=["ost2"], writes=["ost1"])
            S.v("dve", "tensor_tensor", rr[:], rr[:], msk[:], ALU.add, reads=["ost2", "ost1"], writes=["ost2"])
            S.v("dve", "tensor_scalar", msk[:], rr[:], PI, -2 * PI, ALU.is_gt, ALU.mult, reads=["ost2"], writes=["ost1"])
            S.v("dve", "tensor_tensor", rr[:], rr[:], msk[:], ALU.add, reads=["ost2", "ost1"], writes=["ost2"])
            S.v("dve", "tensor_scalar", rr[:], rr[:], 3.14159, -3.14159, ALU.min, ALU.max, reads=["ost2"], writes=["ost2"])
            if scale_ap is None:
                S.act(dst[:], rr[:], AF.Sin, reads=["ost2"], writes=[tag])
            else:
                S.act(dst[:], rr[:], AF.Sin, scale=scale_ap, reads=["ost2", "cst"], writes=[tag])

        reduce_sin(cosT, PI / 2, None, "stash")
        reduce_sin(sinT, 0.0, cst[0:64, 1:2], "stash")

        wb, wk = load_w(w_kr, 0, 128)
        o1 = project(wb, wk, 0, 64)
        o2 = project(wb, wk, 64, 64)
        for tt in range(NT):
            sl = slice(tt * TT, (tt + 1) * TT)
            S.v("dve", "tensor_tensor", rr[:, sl], o1[tt][0][0:64, 0:TT], cosT[:, sl], ALU.mult, reads=[o1[tt][1], "stash"], writes=["ost2"])
            S.v("dve", "tensor_tensor", kf[:, sl], o2[tt][0][0:64, 0:TT], sinT[:, sl], ALU.mult, reads=[o2[tt][1], "stash"], writes=["ost0"])
        S.v("dve", "tensor_tensor", obf[0][0:64, :], rr[:], kf[:], ALU.add, reads=["ost2", "ost0"], writes=["obf0"])
        S.dma("pool", "ob0", [lambda e: e.dma_start(out=o_kpe[:, :], in_=obf[0][0:64, :])], reads=["obf0"], writes=["o_kpe"])

        S.dma("sp", "w0", [lambda e: e.dma_start(out=wst[0][:, :, 0:16], in_=w_ab[:, :].rearrange("(kc p) c -> p kc c", p=128))],
              writes=["wst0"])
        for kc in range(KC):
            S.v("dve", "tensor_scalar", abw[:, kc, :], wst[0][:, kc, 0:16], gm[:, kc:kc + 1], None, ALU.mult,
                reads=["wst0", "gm"], writes=["abw"])
        for s in range(NS):
            pt, pk = P[s % 8], f"P{s % 8}"
            for kc in range(KC):
                S.mm(pt[:, 0:16], xnT[:, kc, s * 128:(s + 1) * 128], abw[:, kc, :], start=(kc == 0), stop=(kc == KC - 1),
                     reads=["xnT", "abw"], writes=[pk])
            S.v("dve", "tensor_copy", absb[:, s, :], pt[:, 0:16], reads=[pk], writes=["absb"])
        S.dma("sp", "oab", [lambda e: e.dma_start(out=o_ab[:, :].rearrange("(s p) c -> p s c", p=128), in_=absb[:])],
              reads=["absb"], writes=["o_ab"])

        S.op("sp", None, reads=["o_pool", "o_qkvz", "o_hglu", "o_cq", "o_ckv", "o_kpe", "o_ab", "od"])
        S.emit()
    return nc

import contextlib, math
import numpy as np

PI = math.pi
EPS = 1e-6
SCALE = 192 ** -0.5


def build_B(S, NB=2, do_gdn=True, do_mla=True):
    C = 128
    NCH = S // C
    SEG = min(512, S)
    NCS = SEG // C
    NSEG = S // SEG
    TQ = min(512, S)
    NTQ = S // TQ
    NT = NB * S
    nc = bass.Bass("TRN2", target_bir_lowering=False)

    def din(name, shape, dt=F32):
        return nc.dram_tensor(name, shape, dt, kind="ExternalInput").ap()

    def dout(name, shape, dt=F32):
        return nc.dram_tensor(name, shape, dt, kind="ExternalOutput").ap()

    gq = din("gq", [128, NT]); gk = din("gk", [128, NT]); gv = din("gv", [128, NT]); gz = din("gz", [128, NT])
    gab = din("gab", [NT, 2])
    gcw = din("gcw", [128, 12])
    gsc = din("gsc", [128, 4])
    masks = din("masks", [128, 5 * 128])
    cqn = din("cqn", [512, NT], BF16); ckvn = din("ckvn", [512, NT], BF16); kpe = din("kpe", [64, NT], BF16)
    wq = din("wq", [512, 256])
    wkv = din("wkv", [512, 256])
    pos = din("pos", [1, NT], I32)
    consts = din("consts", [128, 8])
    o_gdn = dout("o_gdn", [128, NT], BF16)
    o_mla = dout("o_mla", [128, NT], BF16)

    S_ = Sched(nc)
    with contextlib.ExitStack() as st:
        def sb(name, shape, dt=F32):
            return st.enter_context(nc.sbuf_tensor(name, shape, dt))

        def ps(name, shape, dt=F32):
            return st.enter_context(nc.psum_tensor(name, shape, dt))

        def dve(name, *a, r=(), w=(), **k):
            return S_.v("dve", name, *a, reads=r, writes=w, **k)

        def act(out, in_, func, r=(), w=(), **k):
            return S_.act(out, in_, func, reads=r, writes=w, **k)

        def mm(out, lhsT, rhs, start=True, stop=True, r=(), w=()):
            return S_.mm(out, lhsT, rhs, start=start, stop=stop, reads=r, writes=w)

        mks = [sb(f"mk{i}", [128, 128]) for i in range(5)]
        ident = mks[0][:, :]; triU = mks[1][:, :]; mSL = mks[2][:, :]; mU = mks[3][:, :]; mSU = mks[4][:, :]
        mkb = sb("mkb", [128, 2 * 128], BF16)
        identb = mkb[:, 0:128]; mUb = mkb[:, 128:256]
        cw = sb("cw", [128, 12]); sc = sb("sc", [128, 4]); cst = sb("cst", [128, 8])
        onesf = sb("onesf", [128, 128]); epsT = sb("epsT", [128, 1]); negA = sb("negA", [128, 1])
        S_.dma("sp", "cload", [
            *[(lambda e, i=i: e.dma_start(out=mks[i][:], in_=masks[:, i * 128:(i + 1) * 128])) for i in range(5)],
            lambda e: e.dma_start(out=cw[:], in_=gcw[:, :]),
            lambda e: e.dma_start(out=sc[:], in_=gsc[:, :]),
            lambda e: e.dma_start(out=cst[:], in_=consts[:, :]),
        ], writes=["mk", "cw", "sc", "cst"])
        dve("tensor_copy", mkb[:, 0:128], ident, r=["mk"], w=["mkb"])
        dve("tensor_copy", mkb[:, 128:256], mU, r=["mk"], w=["mkb"])
        S_.v("pool", "memset", onesf[:], 1.0, writes=["onesf"])
        S_.v("pool", "memset", epsT[:], EPS, writes=["epsT"])
        act(negA[:], sc[:, 0:1], AF.Exp, r=["sc"], w=["negA"])
        dve("tensor_scalar", negA[:], negA[:], -1.0, None, ALU.mult, r=["negA"], w=["negA"])

        PG = [ps(f"PG{i}", [128, 512]) for i in range(3)]
        PS = [ps(f"PS{i}", [128, 512]) for i in range(2)]
        PO = [ps(f"PO{i}", [128, 512]) for i in range(2)]
        PM = ps("PM", [128, 512])
        gslot = [0]

        def gps():
            i = gslot[0] % 3
            gslot[0] += 1
            return PG[i][:, 0:128], f"PG{i}"

        import os
        gst = [0]
        GSTOP = int(os.environ.get('GSTOP', '100000'))
        def stage_stop():
            gst[0] += 1
            return gst[0] >= GSTOP
        def gdn_stream(b):
            t0 = b * S
            raw = [sb(f"raw{b}_{i}", [128, 3 + SEG]) for i in range(3)]
            cv = [sb(f"cv{b}_{i}", [128, SEG]) for i in range(3)]
            zs = sb(f"zs{b}", [128, SEG]); tmp = sb(f"gtmp{b}", [128, SEG]); rin = sb(f"rin{b}", [128, SEG])
            og = sb(f"og{b}", [128, SEG], BF16)
            ab = sb(f"ab{b}", [128, NCS, 2])
            gg = sb(f"gg{b}", [128, NCS]); beta = sb(f"beta{b}", [128, NCS]); gc = sb(f"gc{b}", [128, NCS])
            gl = sb(f"gl{b}", [128, NCS]); eg = sb(f"eg{b}", [128, NCS]); egl = sb(f"egl{b}", [128, NCS])
            edl = sb(f"edl{b}", [128, NCS]); bge = sb(f"bge{b}", [128, NCS])
            St = sb(f"St{b}", [128, 128])
            dve("memset", St[:], 0.0, w=[f"St{b}"])
            NBUF = 2
            names = ["kb", "kbg", "kdec", "vb", "kbT", "Bm", "E", "decT", "decTs", "U", "L", "U2", "L2", "P", "P2",
                     "nwT", "qkT", "vnew", "oB", "otok", "on", "st1", "st2"]
            tl = {n: [sb(f"{n}{b}_{i}", [128, 128]) for i in range(NBUF)] for n in names}

            for sg in range(NSEG):
                s0 = t0 + sg * SEG
                K = lambda n: f"{n}{b}"
                fns = []
                for i, src in enumerate((gq, gk, gv)):
                    if sg == 0:
                        dve("memset", raw[i][:, 0:3], 0.0, w=[f"raw{b}_{i}"])
                        fns.append(lambda e, i=i, src=src, s0=s0: e.dma_start(out=raw[i][:, 3:3 + SEG], in_=src[:, s0:s0 + SEG]))
                    else:
                        fns.append(lambda e, i=i, src=src, s0=s0: e.dma_start(out=raw[i][:, 0:3 + SEG], in_=src[:, s0 - 3:s0 + SEG]))
                fns.append(lambda e, s0=s0: e.dma_start(out=zs[:], in_=gz[:, s0:s0 + SEG]))
                fns.append(lambda e, s0=s0: e.dma_start(out=ab[:], in_=gab[s0:s0 + SEG, :].rearrange("(n p) c -> p n c", p=128)))
                S_.dma("sp", f"gld{b}", fns, writes=[f"raw{b}_0", f"raw{b}_1", f"raw{b}_2", K("zs"), K("ab")])
                yield
                if stage_stop(): return
                for i in range(3):
                    rk = f"raw{b}_{i}"
                    dve("tensor_scalar", tmp[:], raw[i][:, 0:SEG], cw[:, 4 * i:4 * i + 1], None, ALU.mult, r=[rk, "cw"], w=[K("gtmp")])
                    for j in range(1, 4):
                        dve("scalar_tensor_tensor", tmp[:], raw[i][:, j:j + SEG], cw[:, 4 * i + j:4 * i + j + 1], tmp[:], ALU.mult, ALU.add,
                            r=[rk, "cw", K("gtmp")], w=[K("gtmp")])
                    act(cv[i][:], tmp[:], AF.Silu, r=[K("gtmp")], w=[f"cv{b}_{i}"])
                    yield
                    if stage_stop(): return
                act(zs[:], zs[:], AF.Silu, r=[K("zs")], w=[K("zs")])
                for i in range(2):
                    ck = f"cv{b}_{i}"
                    for t5 in range(0, SEG, 512):
                        w5 = min(512, SEG - t5)
                        act(tmp[:, t5:t5 + w5], cv[i][:, t5:t5 + w5], AF.Square, r=[ck], w=[K("gtmp")])
                        mm(PM[:, 0:w5], onesf[:], tmp[:, t5:t5 + w5], r=["onesf", K("gtmp")], w=["PM"])
                        act(rin[:, t5:t5 + w5], PM[:, 0:w5], AF.Sqrt, bias=epsT[:, 0:1], r=["PM", "epsT"], w=[K("rin")])
                    dve("reciprocal", rin[:], rin[:], r=[K("rin")], w=[K("rin")])
                    if i == 0:
                        dve("scalar_tensor_tensor", cv[i][:], cv[i][:], 128 ** -0.5, rin[:], ALU.mult, ALU.mult, r=[ck, K("rin")], w=[ck])
                    else:
                        dve("tensor_tensor", cv[i][:], cv[i][:], rin[:], ALU.mult, r=[ck, K("rin")], w=[ck])
                    yield
                    if stage_stop(): return
                act(gg[:], ab[:, :, 0], AF.Exp, bias=sc[:, 1:2], r=[K("ab"), "sc"], w=[K("gg")])
                act(gg[:], gg[:], AF.Ln, bias=1.0, r=[K("gg")], w=[K("gg")])
                dve("tensor_scalar", gg[:], gg[:], negA[:, 0:1], None, ALU.mult, r=[K("gg"), "negA"], w=[K("gg")])
                act(beta[:], ab[:, :, 1], AF.Sigmoid, r=[K("ab")], w=[K("beta")])
                mm(PM[:, 0:NCS], triU, gg[:], r=["mk", K("gg")], w=["PM"])
                dve("tensor_copy", gc[:], PM[:, 0:NCS], r=["PM"], w=[K("gc")])
                mm(PM[:, 0:NCS], onesf[:], gg[:], r=["onesf", K("gg")], w=["PM"])
                dve("tensor_copy", gl[:], PM[:, 0:NCS], r=["PM"], w=[K("gl")])
                act(eg[:], gc[:], AF.Exp, r=[K("gc")], w=[K("eg")])
                act(egl[:], gl[:], AF.Exp, r=[K("gl")], w=[K("egl")])
                dve("tensor_tensor", edl[:], gl[:], gc[:], ALU.subtract, r=[K("gl"), K("gc")], w=[K("edl")])
                act(edl[:], edl[:], AF.Exp, r=[K("edl")], w=[K("edl")])
                dve("tensor_tensor", bge[:], beta[:], eg[:], ALU.mult, r=[K("beta"), K("eg")], w=[K("bge")])
                yield
                if stage_stop(): return
                for n in range(NCS):
                    bi = n % NBUF
                    T = lambda nm: tl[nm][bi]
                    TK = lambda nm: f"{nm}{b}_{bi}"
                    cs = slice(n * C, (n + 1) * C)
                    qT = cv[0][:, cs]; kT = cv[1][:, cs]; vT = cv[2][:, cs]
                    qk_, kk_, vk_ = f"cv{b}_0", f"cv{b}_1", f"cv{b}_2"
                    col = lambda t_: t_[:, n:n + 1]
                    p1, p1k = gps()
                    S_.tr(p1, kT, ident, reads=[kk_, "mk"], writes=[p1k])
                    dve("tensor_scalar", T("kb")[:], p1, col(beta), None, ALU.mult, r=[p1k, K("beta")], w=[TK("kb")])
                    dve("tensor_scalar", T("kbg")[:], p1, col(bge), None, ALU.mult, r=[p1k, K("bge")], w=[TK("kbg")])
                    dve("tensor_scalar", T("kdec")[:], p1, col(edl), None, ALU.mult, r=[p1k, K("edl")], w=[TK("kdec")])
                    p2, p2k = gps()
                    S_.tr(p2, vT, ident, reads=[vk_, "mk"], writes=[p2k])
                    dve("tensor_scalar", T("vb")[:], p2, col(beta), None, ALU.mult, r=[p2k, K("beta")], w=[TK("vb")])
                    yield
                    if stage_stop(): return
                    GSUB = int(os.environ.get('GSUB', '99'))
                    p3, p3k = gps()
                    S_.tr(p3, T("kb")[:], ident, reads=[TK("kb"), "mk"], writes=[p3k])
                    if GSUB < 1: return
                    act(T("kbT")[:], p3, AF.Copy, r=[p3k], w=[TK("kbT")])
                    if GSUB < 2: return
                    act(T("Bm")[:], mSL, AF.Copy, scale=col(gg), r=["mk", K("gg")], w=[TK("Bm")])
                    if GSUB < 3: return
                    p4, p4k = gps()
                    mm(p4, T("Bm")[:], triU, r=[TK("Bm"), "mk"], w=[p4k])
                    if GSUB < 4: return
                    act(T("E")[:], p4, AF.Exp, r=[p4k], w=[TK("E")])
                    if GSUB < 5: return
                    dve("tensor_tensor", T("decT")[:], T("E")[:], mU, ALU.mult, r=[TK("E"), "mk"], w=[TK("decT")])
                    dve("tensor_tensor", T("decTs")[:], T("E")[:], mSU, ALU.mult, r=[TK("E"), "mk"], w=[TK("decTs")])
                    yield
                    if stage_stop(): return
                    G2 = int(os.environ.get('GSUB2', '99'))
                    p5, p5k = gps()
                    mm(p5, kT, T("kbT")[:], r=[kk_, TK("kbT")], w=[p5k])
                    if G2 < 1: return
                    dve("tensor_tensor", T("U")[:], p5, T("decTs")[:], ALU.mult, r=[p5k, TK("decTs")], w=[TK("U")])
                    if G2 < 2: return
                    p6, p6k = gps()
                    S_.tr(p6, T("U")[:], ident, reads=[TK("U"), "mk"], writes=[p6k])
                    act(T("L")[:], p6, AF.Copy, r=[p6k], w=[TK("L")])
                    if G2 < 3: return
                    dve("tensor_tensor", T("P")[:], ident, T("U")[:], ALU.subtract, r=["mk", TK("U")], w=[TK("P")])
                    if G2 < 4: return
                    p7, p7k = gps()
                    mm(p7, kT, qT, r=[kk_, qk_], w=[p7k])
                    if G2 < 5: return
                    dve("tensor_tensor", T("qkT")[:], p7, T("decT")[:], ALU.mult, r=[p7k, TK("decT")], w=[TK("qkT")])
                    yield
                    if stage_stop(): return
                    Uc, Lc, Pc = "U", "L", "P"
                    Un, Ln, Pn = "U2", "L2", "P2"
                    for step in range(6):
                        last = step == 5
                        pa, pak = gps()
                        mm(pa, T(Uc)[:], T(Lc)[:], r=[TK(Uc), TK(Lc)], w=[pak])
                        act(T(Ln)[:], pa, AF.Copy, r=[pak], w=[TK(Ln)])
                        if not last:
                            pb, pbk = gps()
                            mm(pb, T(Lc)[:], T(Uc)[:], r=[TK(Uc), TK(Lc)], w=[pbk])
                            dve("tensor_copy", T(Un)[:], pb, r=[pbk], w=[TK(Un)])
                        pc, pck = gps()
                        mm(pc, T(Ln)[:], T(Pc)[:], r=[TK(Ln), TK(Pc)], w=[pck])
                        dve("tensor_tensor", T(Pn)[:], pc, T(Pc)[:], ALU.add, r=[pck, TK(Pc)], w=[TK(Pn)])
                        Uc, Un = Un, Uc
                        Lc, Ln = Ln, Lc
                        Pc, Pn = Pn, Pc
                        yield
                        if stage_stop(): return
                    Tt, Ttk = T(Pc), TK(Pc)
                    p8, p8k = gps()
                    mm(p8, T("kbg")[:], Tt[:], r=[TK("kbg"), Ttk], w=[p8k])
                    act(T("nwT")[:], p8, AF.Copy, scale=-1.0, r=[p8k], w=[TK("nwT")])
                    yield
                    if stage_stop(): return
                    p9, p9k = gps()
                    mm(p9, Tt[:], T("vb")[:], start=True, stop=False, r=[Ttk, TK("vb")], w=[p9k])
                    mm(p9, T("nwT")[:], St[:], start=False, stop=True, r=[TK("nwT"), K("St")], w=[p9k])
                    act(T("vnew")[:], p9, AF.Copy, r=[p9k], w=[TK("vnew")])
                    pA, pAk = gps()
                    mm(pA, qT, St[:], r=[qk_, K("St")], w=[pAk])
                    pB, pBk = gps()
                    mm(pB, T("qkT")[:], T("vnew")[:], r=[TK("qkT"), TK("vnew")], w=[pBk])
                    act(T("oB")[:], pB, AF.Copy, r=[pBk], w=[TK("oB")])
                    dve("scalar_tensor_tensor", T("otok")[:], pA, col(eg), T("oB")[:], ALU.mult, ALU.add,
                        r=[pAk, K("eg"), TK("oB")], w=[TK("otok")])
                    pC, pCk = gps()
                    mm(pC, T("kdec")[:], T("vnew")[:], r=[TK("kdec"), TK("vnew")], w=[pCk])
                    dve("scalar_tensor_tensor", St[:], St[:], col(egl), pC, ALU.mult, ALU.add, r=[K("St"), K("egl"), pCk], w=[K("St")])
                    yield
                    if stage_stop(): return
                    dve("memset", T("st1")[:, 0:1], 0.0, w=[TK("st1")])
                    act(T("on")[:], T("otok")[:], AF.Square, accum_out=T("st1")[:, 0:1], r=[TK("otok")], w=[TK("on"), TK("st1")])
                    act(T("st1")[:, 0:1], T("st1")[:, 0:1], AF.Sqrt, scale=1.0 / 128, bias=epsT[:, 0:1], r=[TK("st1"), "epsT"], w=[TK("st1")])
                    dve("reciprocal", T("st1")[:, 0:1], T("st1")[:, 0:1], r=[TK("st1")], w=[TK("st1")])
                    dve("tensor_scalar", T("on")[:], T("otok")[:], T("st1")[:, 0:1], None, ALU.mult, r=[TK("otok"), TK("st1")], w=[TK("on")])
                    pD, pDk = gps()
                    S_.tr(pD, T("on")[:], ident, reads=[TK("on"), "mk"], writes=[pDk])
                    dve("scalar_tensor_tensor", og[:, cs], pD, sc[:, 2:3], zs[:, cs], ALU.mult, ALU.mult, r=[pDk, "sc", K("zs")], w=[K("og")])
                    yield
                    if stage_stop(): return
                S_.dma("pool", f"gst{b}", [lambda e, s0=s0: e.dma_start(out=o_gdn[:, s0:s0 + SEG], in_=og[:])], reads=[K("og")], writes=["o_gdn"])
                yield
                if stage_stop(): return

        def mla_stream():
            wst = sb("wst", [128, 4, 256]); wqb = sb("wqb", [128, 4, 256], BF16); wkvb = sb("wkvb", [128, 4, 256], BF16)
            S_.dma("sp", "wl", [lambda e: e.dma_start(out=wst[:], in_=wq[:, :].rearrange("(kc p) c -> p kc c", p=128))], writes=["wst"])
            dve("tensor_copy", wqb[:], wst[:], r=["wst"], w=["wqb"])
            S_.dma("sp", "wl", [lambda e: e.dma_start(out=wst[:], in_=wkv[:, :].rearrange("(kc p) c -> p kc c", p=128))], writes=["wst"])
            dve("tensor_copy", wkvb[:], wst[:], r=["wst"], w=["wkvb"])
            kT = sb("kT", [128, S], BF16); ka = sb("ka", [65, S], BF16); vt = sb("vt", [128, S // 128, 129], BF16)
            cq = [sb(f"cq{i}", [128, 4, TQ], BF16) for i in range(2)]
            ckv = [sb(f"ckv{i}", [128, 4, TQ], BF16) for i in range(2)]
            qTn = sb("qTn", [128, TQ], BF16); qa = sb("qa", [65, TQ], BF16)
            sqf = sb("sqf", [128, TQ]); sqr = sb("sqr", [64, TQ])
            sel = sb("sel", [128, 65]); kmax = sb("kmax", [65, 1]); kmt = sb("kmt", [65, 1]); mrow = sb("mrow", [65, TQ])
            pT = [sb(f"pT{i}", [128, TQ], BF16) for i in range(3)]
            onb = sb("onb", [128, 128], BF16); rcp = sb("rcp", [128, 1]); om = sb("om", [128, TQ], BF16)
            posi = sb("posi", [64, TQ], I32); ang = sb("ang", [64, TQ]); cosT = sb("cosT", [64, TQ]); sinT = sb("sinT", [64, TQ])
            rr = sb("rr", [64, TQ]); ki = sb("ki", [64, TQ], I32); kf = sb("kf", [64, TQ]); msk = sb("msk", [64, TQ])
            S_.v("pool", "memset", sel[:], 0.0, writes=["sel"])
            S_.v("pool", "memset", sel[:, 64:65], 1.0, writes=["sel"])
            dve("memset", vt[:, :, 128:129], 1.0, w=["vt"])
            dve("memset", ka[64:65, :], 1.0, w=["ka"])
            pctr = [0]

            def reduce_sin(dst, shift, scale_ap, tag):
                dve("tensor_scalar", rr[:], ang[:], 1.0 / (2 * PI), (shift + PI) / (2 * PI), ALU.mult, ALU.add, r=["ang"], w=["rr"])
                dve("tensor_copy", ki[:], rr[:], r=["rr"], w=["ki"])
                dve("tensor_copy", kf[:], ki[:], r=["ki"], w=["kf"])
                dve("tensor_scalar", rr[:], ang[:], shift, None, ALU.add, r=["ang"], w=["rr"])
                dve("scalar_tensor_tensor", rr[:], kf[:], -2 * PI, rr[:], ALU.mult, ALU.add, r=["kf", "rr"], w=["rr"])
                dve("tensor_scalar", msk[:], rr[:], -PI, 2 * PI, ALU.is_lt, ALU.mult, r=["rr"], w=["msk"])
                dve("tensor_tensor", rr[:], rr[:], msk[:], ALU.add, r=["rr", "msk"], w=["rr"])
                dve("tensor_scalar", msk[:], rr[:], PI, -2 * PI, ALU.is_gt, ALU.mult, r=["rr"], w=["msk"])
                dve("tensor_tensor", rr[:], rr[:], msk[:], ALU.add, r=["rr", "msk"], w=["rr"])
                dve("tensor_scalar", rr[:], rr[:], 3.14159, -3.14159, ALU.min, ALU.max, r=["rr"], w=["rr"])
                if scale_ap is None:
                    act(dst[:], rr[:], AF.Sin, r=["rr"], w=[tag])
                else:
                    act(dst[:], rr[:], AF.Sin, scale=scale_ap, r=["rr", "cst"], w=[tag])

            for b in range(NB):
                t0 = b * S
                dve("memset", kmax[64:65, :], 0.0, w=["kmax"])
                for tq in range(NTQ):
                    q0 = tq * TQ
                    g0 = t0 + q0
                    li = (b * NTQ + tq) % 2
                    S_.dma("sp", f"lat{li}", [
                        lambda e, li=li, g0=g0: e.dma_start(out=cq[li][:], in_=cqn[:, g0:g0 + TQ].rearrange("(kc p) t -> p kc t", p=128)),
                        lambda e, li=li, g0=g0: e.dma_start(out=ckv[li][:], in_=ckvn[:, g0:g0 + TQ].rearrange("(kc p) t -> p kc t", p=128)),
                    ], writes=[f"cq{li}", f"ckv{li}"])
                    S_.dma("sp", "kpl", [
                        lambda e, g0=g0, q0=q0: e.dma_start(out=ka[0:64, q0:q0 + TQ], in_=kpe[:, g0:g0 + TQ]),
                        lambda e, g0=g0: e.dma_start(out=posi[:], in_=pos[0:1, g0:g0 + TQ].broadcast_to([64, TQ])),
                    ], writes=["ka", "posi"])
                    dve("tensor_copy", ang[:], posi[:], r=["posi"], w=["ang"])
                    dve("tensor_scalar", ang[:], ang[:], cst[0:64, 0:1], None, ALU.mult, r=["ang", "cst"], w=["ang"])
                    reduce_sin(cosT, PI / 2, None, "cosT")
                    reduce_sin(sinT, 0.0, cst[0:64, 1:2], "sinT")
                    yield
                    for kc in range(4):
                        mm(PM[:, 0:TQ], wkvb[:, kc, 0:128], ckv[li][:, kc, :], start=(kc == 0), stop=(kc == 3), r=["wkvb", f"ckv{li}"], w=["PM"])
                    act(kT[:, q0:q0 + TQ], PM[:, 0:TQ], AF.Copy, r=["PM"], w=["kT"])
                    act(sqf[:], PM[:, 0:TQ], AF.Square, r=["PM"], w=["sqf"])
                    dve("tensor_tensor", sqr[:], ka[0:64, q0:q0 + TQ], ka[0:64, q0:q0 + TQ], ALU.mult, r=["ka"], w=["sqr"])
                    mm(PM[0:65, 0:TQ], sel[:, :], sqf[:], start=True, stop=False, r=["sel", "sqf"], w=["PM"])
                    mm(PM[0:65, 0:TQ], sel[0:64, :], sqr[:], start=False, stop=True, r=["sel", "sqr"], w=["PM"])
                    dve("reduce_max", kmt[64:65, :], PM[64:65, 0:TQ], AX.X, r=["PM"], w=["kmt"])
                    dve("tensor_tensor", kmax[64:65, :], kmax[64:65, :], kmt[64:65, :], ALU.max, r=["kmax", "kmt"], w=["kmax"])
                    yield
                    for sbk in range(TQ // 128):
                        for kc in range(4):
                            mm(PM[:, 0:128], ckv[li][:, kc, sbk * 128:(sbk + 1) * 128], wkvb[:, kc, 128:256], start=(kc == 0), stop=(kc == 3),
                               r=["wkvb", f"ckv{li}"], w=["PM"])
                        dve("tensor_copy", vt[:, q0 // 128 + sbk, 0:128], PM[:, 0:128], r=["PM"], w=["vt"])
                    yield
                    for kc in range(4):
                        mm(PM[:, 0:TQ], wqb[:, kc, 0:128], cq[li][:, kc, :], start=(kc == 0), stop=(kc == 3), r=["wqb", f"cq{li}"], w=["PM"])
                    act(qTn[:], PM[:, 0:TQ], AF.Copy, r=["PM"], w=["qTn"])
                    act(sqf[:], PM[:, 0:TQ], AF.Square, r=["PM"], w=["sqf"])
                    for kc in range(4):
                        mm(PM[0:64, 0:TQ], wqb[:, kc, 128:192], cq[li][:, kc, :], start=(kc == 0), stop=(kc == 3), r=["wqb", f"cq{li}"], w=["PM"])
                    dve("tensor_tensor", rr[:], PM[0:64, 0:TQ], cosT[:], ALU.mult, r=["PM", "cosT"], w=["rr"])
                    for kc in range(4):
                        mm(PM[0:64, 0:TQ], wqb[:, kc, 192:256], cq[li][:, kc, :], start=(kc == 0), stop=(kc == 3), r=["wqb", f"cq{li}"], w=["PM"])
                    dve("tensor_tensor", kf[:], PM[0:64, 0:TQ], sinT[:], ALU.mult, r=["PM", "sinT"], w=["kf"])
                    dve("tensor_tensor", rr[:], rr[:], kf[:], ALU.add, r=["rr", "kf"], w=["rr"])
                    dve("tensor_copy", qa[0:64, :], rr[:], r=["rr"], w=["qa"])
                    dve("tensor_tensor", sqr[:], rr[:], rr[:], ALU.mult, r=["rr"], w=["sqr"])
                    mm(PM[0:65, 0:TQ], sel[:, :], sqf[:], start=True, stop=False, r=["sel", "sqf"], w=["PM"])
                    mm(PM[0:65, 0:TQ], sel[0:64, :], sqr[:], start=False, stop=True, r=["sel", "sqr"], w=["PM"])
                    dve("tensor_scalar", mrow[64:65, :], PM[64:65, 0:TQ], kmax[64:65, 0:1], None, ALU.mult, r=["PM", "kmax"], w=["mrow"])
                    act(mrow[64:65, :], mrow[64:65, :], AF.Sqrt, r=["mrow"], w=["mrow"])
                    dve("tensor_scalar", qa[64:65, :], mrow[64:65, :], -1.0, None, ALU.mult, r=["mrow"], w=["qa"])
                    yield
                    nkb = (q0 + TQ) // 128
                    nqb = TQ // 128
                    first = [True, True]
                    for kb in range(nkb):
                        n0 = max(0, kb * 128 - q0)
                        N = TQ - n0
                        pi_ = pctr[0] % 2
                        pti = pctr[0] % 3
                        pctr[0] += 1
                        sp_, spk = PS[pi_], f"PS{pi_}"
                        mm(sp_[:, 0:N], kT[:, kb * 128:(kb + 1) * 128], qTn[:, n0:TQ], start=True, stop=False, r=["kT", "qTn"], w=[spk])
                        mm(sp_[:, 0:N], ka[0:65, kb * 128:(kb + 1) * 128], qa[0:65, n0:TQ], start=False, stop=True, r=["ka", "qa"], w=[spk])
                        act(pT[pti][:, 0:N], sp_[:, 0:N], AF.Exp, scale=SCALE, r=[spk], w=[f"pT{pti}"])
                        if kb * 128 >= q0:
                            dve("tensor_tensor", pT[pti][:, 0:128], pT[pti][:, 0:128], mUb, ALU.mult, r=[f"pT{pti}", "mkb"], w=[f"pT{pti}"])
                        for qb in range(n0 // 128, nqb):
                            bank = qb // 2
                            oc = (qb % 2) * 129
                            c0 = qb * 128 - n0
                            lastkb = (q0 + qb * 128) // 128
                            S_.mm(PO[bank][:, oc:oc + 129], pT[pti][:, c0:c0 + 128], vt[:, kb, :], start=first[bank],
                                  stop=(kb == lastkb and qb % 2 == 1), reads=[f"pT{pti}", "vt"], writes=[f"PO{bank}"], skip_group_check=True)
                            first[bank] = False
                        if kb % 2 == 1:
                            yield
                    for qb in range(nqb):
                        bank = qb // 2
                        oc = (qb % 2) * 129
                        dve("reciprocal", rcp[:], PO[bank][:, oc + 128:oc + 129], r=[f"PO{bank}"], w=["rcp"])
                        dve("tensor_scalar", onb[:], PO[bank][:, oc:oc + 128], rcp[:, 0:1], None, ALU.mult, r=[f"PO{bank}", "rcp"], w=["onb"])
                        pv = PM[:].bitcast(BF16)
                        S_.tr(pv[:, 0:128], onb[:], identb, reads=["onb", "mkb"], writes=["PM"])
                        act(om[:, qb * 128:(qb + 1) * 128], pv[:, 0:128], AF.Copy, r=["PM"], w=["om"])
                    S_.dma("pool", "mst", [lambda e, g0=g0: e.dma_start(out=o_mla[:, g0:g0 + TQ], in_=om[:])], reads=["om"], writes=["o_mla"])
                    yield

        gens = []
        if do_gdn:
            gens += [gdn_stream(b) for b in range(NB)]
        if do_mla:
            gens.append(mla_stream())
        while gens:
            for g in list(gens):
                try:
                    next(g)
                except StopIteration:
                    gens.remove(g)
        S_.op("sp", None, reads=["o_gdn", "o_mla"])
        S_.emit()
    return nc

import contextlib, math
import numpy as np

D = 2048
KC = 16
EPS = 1e-6
LN_EPS = 1e-5
HALO = 32
FFN = 5632
NFC = FFN // 128


class Ctx:
    def __init__(self, nc, st):
        self.nc = nc
        self.st = st
        self.S = Sched(nc)
        self.P = [st.enter_context(nc.psum_tensor(f"P{i}", [128, 512], F32)) for i in range(8)]
        self.pctr = 0
        self.ectr = 0

    def sb(self, name, shape, dt=F32):
        return self.st.enter_context(self.nc.sbuf_tensor(name, shape, dt))

    def pb(self):
        i = self.pctr % 8
        self.pctr += 1
        return self.P[i], f"P{i}"

    def dve(self, name, *a, r=(), w=(), **k):
        return self.S.v("dve", name, *a, reads=r, writes=w, **k)

    def act(self, out, in_, func, r=(), w=(), **k):
        return self.S.act(out, in_, func, reads=r, writes=w, **k)

    def mm(self, out, lhsT, rhs, start=True, stop=True, r=(), w=(), **k):
        return self.S.mm(out, lhsT, rhs, start=start, stop=stop, reads=r, writes=w, **k)

    def copy(self, dst, src, r, w):
        self.ectr += 1
        if self.ectr % 2 == 0:
            self.act(dst, src, AF.Copy, r=r, w=w)
        else:
            self.dve("tensor_copy", dst, src, r=r, w=w)


def norm_transpose(cx, xrows_ap, nrows, xt, xk, xs, junk, ss, rstd, epsT, idb, gm, dstT, col0, load=True):
    S = cx.S
    if load:
        S.dma("sp", xk, [lambda e: e.dma_start(out=xt[0:nrows, :], in_=xrows_ap)], writes=[xk])
    cx.dve("memset", ss[0:nrows, 0:1], 0.0, w=["ss"])
    cx.act(junk[0:nrows, :], xt[0:nrows, :], AF.Square, accum_out=ss[0:nrows, 0:1], r=[xk, "ss"], w=["junk", "ss"])
    cx.act(rstd[0:nrows, 0:1], ss[0:nrows, 0:1], AF.Sqrt, scale=1.0 / D, bias=epsT[0:nrows, 0:1], r=["ss", "epsT"], w=["rstd"])
    cx.dve("reciprocal", rstd[0:nrows, 0:1], rstd[0:nrows, 0:1], r=["rstd"], w=["rstd"])
    cx.dve("tensor_scalar", xs[0:nrows, :], xt[0:nrows, :], rstd[0:nrows, 0:1], None, ALU.mult, r=[xk, "rstd"], w=["xs"])
    for h in range(2):
        pt, pk = cx.pb()
        pv = pt[:].bitcast(BF16)
        for j in range(8):
            kc = h * 8 + j
            S.tr(pv[:, j * 128:j * 128 + nrows], xs[0:nrows, kc * 128:(kc + 1) * 128], idb[0:nrows, 0:nrows], reads=["xs", "idb"], writes=[pk])
        for j in range(8):
            kc = h * 8 + j
            src = pv[:, j * 128:j * 128 + nrows]
            dst = dstT[:, kc, col0:col0 + nrows]
            if j % 2 == 0:
                cx.act(dst, src, AF.Copy, scale=gm[:, kc:kc + 1], r=[pk, "gm"], w=["xnT"])
            else:
                cx.dve("tensor_scalar", dst, src, gm[:, kc:kc + 1], None, ALU.mult, r=[pk, "gm"], w=["xnT"])


def build_C1(T):
    TH = min(512, T)
    NTH = T // TH
    W = HALO + TH
    nc = bass.Bass("TRN2", target_bir_lowering=False)

    def din(name, shape, dt=F32):
        return nc.dram_tensor(name, shape, dt, kind="ExternalInput").ap()

    x = din("x", [T, D])
    upool = din("upool", [1024, HALO + T])
    hglu = din("hglu", [1024, HALO + T])
    ogdn = din("ogdn", [1024, T], BF16)
    omla = din("omla", [1024, T], BF16)
    ident_in = din("ident", [128, 128])
    g_mix = din("g_mix", [128, KC])
    cvec = din("cvec", [128, 64])
    cconv = din("cconv", [128, 8 * 31])
    poolw = din("poolw", [8, 128, 256])
    wg = din("wg", [64, 128, KC * 128])
    wo = din("wo", [64, 128, 8 * 128])
    wout = din("wout", [16, 128, KC * 128])
    xmid = nc.dram_tensor("xmid", [T, D], F32, kind="ExternalOutput").ap()

    with contextlib.ExitStack() as st:
        cx = Ctx(nc, st)
        S = cx.S
        sb, dve, act, mm = cx.sb, cx.dve, cx.act, cx.mm
        xnT = sb("xnT", [128, KC, TH], BF16)
        br = [sb(f"br{i}", [128, 8, TH], BF16) for i in range(4)]
        merged = sb("merged", [128, KC, TH], BF16)
        wst = [sb(f"wst{i}", [128, KC * 128]) for i in range(2)]
        wbf = [sb(f"wbf{i}", [128, KC * 128], BF16) for i in range(2)]
        xt = [sb(f"xt{i}", [128, D]) for i in range(2)]
        xs = sb("xs", [128, D], BF16); junk = sb("junk", [128, D], BF16)
        ss = sb("ss", [128, 1]); rstd = sb("rstd", [128, 1]); epsT = sb("epsT", [128, 1]); lnepsT = sb("lnepsT", [128, 1])
        idf = sb("idf", [128, 128]); idb = sb("idb", [128, 128], BF16)
        gm = sb("gm", [128, KC]); cv = sb("cv", [128, 64]); ccv = sb("ccv", [128, 8 * 31])
        onesf = sb("onesf", [128, 128])
        pwst = sb("pwst", [128, 8, 256]); pwb = sb("pwb", [128, 8, 256], BF16)
        ub = [sb(f"ub{i}", [128, W]) for i in range(2)]
        s1 = sb("s1", [128, W]); s2 = sb("s2", [128, W])
        dT = sb("dT", [128, 8, TH], BF16)
        invc = sb("invc", [128, 4, 16]); iot = sb("iot", [128, 16])
        cf = sb("cf", [128, 8, TH])
        sq = sb("sq", [128, TH]); mean = sb("mean", [128, TH]); rs = sb("rs", [128, TH]); tmpf = sb("tmpf", [128, TH])
        gsig = [sb(f"gsig{i}", [128, TH]) for i in range(2)]
        acc = sb("acc", [128, TH])

        S.dma("sp", "cload", [
            lambda e: e.dma_start(out=idf[:], in_=ident_in[:, :]),
            lambda e: e.dma_start(out=gm[:], in_=g_mix[:, :]),
            lambda e: e.dma_start(out=cv[:], in_=cvec[:, :]),
            lambda e: e.dma_start(out=ccv[:], in_=cconv[:, :]),
            lambda e: e.dma_start(out=pwst[:], in_=poolw.rearrange("a p d -> p a d")),
        ], writes=["idf", "gm", "cv", "ccv", "pwst"])
        dve("tensor_copy", idb[:], idf[:], r=["idf"], w=["idb"])
        dve("tensor_copy", pwb[:], pwst[:], r=["pwst"], w=["pwb"])
        S.v("pool", "memset", onesf[:], 1.0, writes=["onesf"])
        S.v("pool", "memset", epsT[:], EPS, writes=["epsT"])
        S.v("pool", "memset", lnepsT[:], LN_EPS, writes=["lnepsT"])
        dve("tensor_copy", iot[:], cv[:, 40:56], r=["cv"], w=["iot"])
        for gi, win in enumerate((2, 4, 8, 16)):
            dve("tensor_scalar", invc[:, gi, :], iot[:], cv[:, 32:33], float(win), ALU.add, ALU.min, r=["iot", "cv"], w=["invc"])
        dve("reciprocal", invc[:], invc[:], r=["invc"], w=["invc"])
        dve("memset", s1[:], 0.0, w=["s1"])
        dve("memset", s2[:], 0.0, w=["s2"])

        wctr = [0]

        def load_w(src_ap, nelem):
            i = wctr[0] % 2
            wctr[0] += 1
            S.dma("sp", f"w{i}", [lambda e: e.dma_start(out=wst[i][:, 0:nelem], in_=src_ap)], writes=[f"wst{i}"])
            cx.copy(wbf[i][:, 0:nelem], wst[i][:, 0:nelem], [f"wst{i}"], [f"wbf{i}"])
            return wbf[i], f"wbf{i}"

        for th in range(NTH):
            t0 = th * TH
            for s in range(TH // 128):
                r0 = t0 + s * 128
                norm_transpose(cx, x[r0:r0 + 128, :], 128, xt[s % 2], f"xt{s % 2}", xs, junk, ss, rstd, epsT, idb, gm, xnT, s * 128)
            S.dma("sp", "brl", [
                lambda e, t0=t0: e.dma_start(out=br[1][:], in_=ogdn[:, t0:t0 + TH].rearrange("(kc p) t -> p kc t", p=128)),
                lambda e, t0=t0: e.dma_start(out=br[3][:], in_=omla[:, t0:t0 + TH].rearrange("(kc p) t -> p kc t", p=128)),
            ], writes=["br1", "br3"])
            for c in range(8):
                gi = c // 2
                u = ub[c % 2]
                uk = f"ub{c % 2}"
                S.dma("sp", uk, [lambda e, u=u, c=c, t0=t0: e.dma_start(out=u[:], in_=upool[c * 128:(c + 1) * 128, t0:t0 + W])], writes=[uk])
                cur, ck = u, uk
                sh = 1
                bufs = [(s1, "s1"), (s2, "s2")]
                for k in range(gi + 1):
                    nxt, nk = bufs[k % 2]
                    dve("tensor_tensor", nxt[:, sh:W], cur[:, sh:W], cur[:, 0:W - sh], ALU.add, r=[ck], w=[nk])
                    cur, ck = nxt, nk
                    sh *= 2
                win = 2 ** (gi + 1)
                dve("scalar_tensor_tensor", dT[:, c, :], cur[:, HALO:W], 1.0 / win, u[:, HALO:W], ALU.mult, ALU.subtract, r=[ck, uk], w=["dT"])
                if th == 0:
                    dve("tensor_tensor", tmpf[:, 0:16], cur[:, HALO:HALO + 16], invc[:, gi, :], ALU.mult, r=[ck, "invc"], w=["tmpf"])
                    dve("tensor_tensor", dT[:, c, 0:16], tmpf[:, 0:16], u[:, HALO:HALO + 16], ALU.subtract, r=["tmpf", uk], w=["dT"])
            for g in range(4):
                for dj in range(2):
                    pt, pk = cx.pb()
                    for kc in range(2):
                        mm(pt[:, 0:TH], pwb[:, g * 2 + kc, dj * 128:(dj + 1) * 128], dT[:, g * 2 + kc, :], start=(kc == 0), stop=(kc == 1),
                           r=["pwb", "dT"], w=[pk])
                    act(br[0][:, g * 2 + dj, :], pt[:, 0:TH], AF.Copy, scale=cv[:, g * 2 + dj:g * 2 + dj + 1], r=[pk, "cv"], w=["br0"])
            for c in range(8):
                h = ub[c % 2]
                hk = f"ub{c % 2}"
                S.dma("sp", hk, [lambda e, h=h, c=c, t0=t0: e.dma_start(out=h[:], in_=hglu[c * 128:(c + 1) * 128, t0:t0 + W])], writes=[hk])
                dve("tensor_scalar", cf[:, c, :], h[:, 2:2 + TH], ccv[:, c * 31:c * 31 + 1], cv[:, 8 + c:9 + c], ALU.mult, ALU.add,
                    r=[hk, "ccv", "cv"], w=["cf"])
                for j in range(1, 31):
                    dve("scalar_tensor_tensor", cf[:, c, :], h[:, 2 + j:2 + j + TH], ccv[:, c * 31 + j:c * 31 + j + 1], cf[:, c, :], ALU.mult, ALU.add,
                        r=[hk, "ccv", "cf"], w=["cf"])
            pm, pmk = cx.pb()
            for c in range(8):
                mm(pm[:, 0:TH], onesf[:], cf[:, c, :], start=(c == 0), stop=(c == 7), r=["onesf", "cf"], w=[pmk])
            act(mean[:], pm[:, 0:TH], AF.Copy, scale=1.0 / 1024, r=[pmk], w=["mean"])
            pv, pvk = cx.pb()
            for c in range(8):
                dve("tensor_tensor", cf[:, c, :], cf[:, c, :], mean[:], ALU.subtract, r=["cf", "mean"], w=["cf"])
                act(sq[:], cf[:, c, :], AF.Square, r=["cf"], w=["sq"])
                mm(pv[:, 0:TH], onesf[:], sq[:], start=(c == 0), stop=(c == 7), r=["onesf", "sq"], w=[pvk])
            act(rs[:], pv[:, 0:TH], AF.Sqrt, scale=1.0 / 1024, bias=lnepsT[:, 0:1], r=[pvk, "lnepsT"], w=["rs"])
            dve("reciprocal", rs[:], rs[:], r=["rs"], w=["rs"])
            for c in range(8):
                dve("tensor_tensor", tmpf[:], cf[:, c, :], rs[:], ALU.mult, r=["cf", "rs"], w=["tmpf"])
                act(br[2][:, c, :], tmpf[:], AF.Silu, scale=cv[:, 16 + c:17 + c], bias=cv[:, 24 + c:25 + c], r=["tmpf", "cv"], w=["br2"])
            for f in range(16):
                for i in range(4):
                    blk = f * 4 + i
                    wgb, wgk = load_w(wg[blk, :, :], KC * 128)
                    pg, pgk = cx.pb()
                    for kc in range(KC):
                        mm(pg[:, 0:TH], wgb[:, kc * 128:(kc + 1) * 128], xnT[:, kc, :], start=(kc == 0), stop=(kc == KC - 1), r=[wgk, "xnT"], w=[pgk])
                    gs = gsig[i % 2]
                    act(gs[:], pg[:, 0:TH], AF.Sigmoid, r=[pgk], w=[f"gsig{i % 2}"])
                    wob, wok = load_w(wo[blk, :, :], 8 * 128)
                    py, pyk = cx.pb()
                    for kc in range(8):
                        mm(py[:, 0:TH], wob[:, kc * 128:(kc + 1) * 128], br[i][:, kc, :], start=(kc == 0), stop=(kc == 7), r=[wok, f"br{i}"], w=[pyk])
                    if i == 0:
                        dve("tensor_tensor", acc[:], py[:, 0:TH], gs[:], ALU.mult, r=[pyk, f"gsig{i % 2}"], w=["acc"])
                    else:
                        dve("tensor_tensor", tmpf[:], py[:, 0:TH], gs[:], ALU.mult, r=[pyk, f"gsig{i % 2}"], w=["tmpf"])
                        if i < 3:
                            dve("tensor_tensor", acc[:], acc[:], tmpf[:], ALU.add, r=["acc", "tmpf"], w=["acc"])
                        else:
                            dve("tensor_tensor", merged[:, f, :], acc[:], tmpf[:], ALU.add, r=["acc", "tmpf"], w=["merged"])
            for s in range(TH // 128):
                r0 = t0 + s * 128
                xb, xk = xt[s % 2], f"xt{s % 2}"
                S.dma("sp", xk, [lambda e, xb=xb, r0=r0: e.dma_start(out=xb[:], in_=x[r0:r0 + 128, :])], writes=[xk])
                for n in range(16):
                    wb_, wk_ = load_w(wout[n, :, :], KC * 128)
                    po, pok = cx.pb()
                    for kc in range(KC):
                        mm(po[:, 0:128], merged[:, kc, s * 128:(s + 1) * 128], wb_[:, kc * 128:(kc + 1) * 128], start=(kc == 0), stop=(kc == KC - 1),
                           r=["merged", wk_], w=[pok])
                    dve("tensor_tensor", xb[:, n * 128:(n + 1) * 128], xb[:, n * 128:(n + 1) * 128], po[:, 0:128], ALU.add, r=[xk, pok], w=[xk])
                S.dma("pool", f"xo{s % 2}", [lambda e, xb=xb, r0=r0: e.dma_start(out=xmid[r0:r0 + 128, :], in_=xb[:])], reads=[xk], writes=["xmid"])
        S.op("sp", None, reads=["xmid"])
        S.emit()
    return nc


def build_C2(T, final):
    TH = min(512, T)
    NTH = T // TH
    nc = bass.Bass("TRN2", target_bir_lowering=False)

    def din(name, shape, dt=F32):
        return nc.dram_tensor(name, shape, dt, kind="ExternalInput").ap()

    xm = din("xm", [2 + T, D])
    ident_in = din("ident", [128, 128])
    g_ffn = din("g_ffn", [128, KC])
    fcv = din("fcv", [128, 88 * 4])
    wup = din("wup", [88, 128, KC * 128])
    wdn = din("wdn", [32, 128, 22 * 128])
    fng = din("fng", [1, D])
    xout = nc.dram_tensor("xout", [T, D], F32, kind="ExternalOutput").ap()

    with contextlib.ExitStack() as st:
        cx = Ctx(nc, st)
        S = cx.S
        sb, dve, act, mm = cx.sb, cx.dve, cx.act, cx.mm
        hnT = sb("xnT", [128, KC, TH], BF16)
        hhT = sb("hhT", [128, KC, 2], BF16)
        actt = sb("actt", [128, NFC, TH], BF16)
        wst = [sb(f"wst{i}", [128, 22 * 128]) for i in range(2)]
        wbf = [sb(f"wbf{i}", [128, 22 * 128], BF16) for i in range(2)]
        xt = [sb(f"xt{i}", [128, D]) for i in range(2)]
        xs = sb("xs", [128, D], BF16); junk = sb("junk", [128, D], BF16)
        ss = sb("ss", [128, 1]); rstd = sb("rstd", [128, 1]); epsT = sb("epsT", [128, 1])
        idf = sb("idf", [128, 128]); idb = sb("idb", [128, 128], BF16)
        gm = sb("gm", [128, KC]); fc = sb("fc", [128, 88 * 4])
        hprev = sb("hprev", [128, 88, 2])
        hext = [sb(f"hext{i}", [128, 2 + TH]) for i in range(2)]
        yg = sb("yg", [128, TH]); yu = sb("yu", [128, TH])
        yfm = sb("yfm", [128, KC, TH])
        fg = sb("fg", [128, D])

        S.dma("sp", "cload", [
            lambda e: e.dma_start(out=idf[:], in_=ident_in[:, :]),
            lambda e: e.dma_start(out=gm[:], in_=g_ffn[:, :]),
            lambda e: e.dma_start(out=fc[:], in_=fcv[:, :]),
            lambda e: e.dma_start(out=fg[:], in_=fng[0:1, :].broadcast_to([128, D])),
        ], writes=["idf", "gm", "fc", "fg"])
        dve("tensor_copy", idb[:], idf[:], r=["idf"], w=["idb"])
        S.v("pool", "memset", epsT[:], EPS, writes=["epsT"])

        wctr = [0]

        def load_w(src_ap, nelem):
            i = wctr[0] % 2
            wctr[0] += 1
            S.dma("sp", f"w{i}", [lambda e: e.dma_start(out=wst[i][:, 0:nelem], in_=src_ap)], writes=[f"wst{i}"])
            cx.copy(wbf[i][:, 0:nelem], wst[i][:, 0:nelem], [f"wst{i}"], [f"wbf{i}"])
            return wbf[i], f"wbf{i}"

        norm_transpose(cx, xm[0:2, :], 2, xt[0], "xt0", xs, junk, ss, rstd, epsT, idb, gm, hhT, 0)

        for th in range(NTH):
            t0 = th * TH
            for s in range(TH // 128):
                r0 = 2 + t0 + s * 128
                norm_transpose(cx, xm[r0:r0 + 128, :], 128, xt[s % 2], f"xt{s % 2}", xs, junk, ss, rstd, epsT, idb, gm, hnT, s * 128)
            for c in range(NFC):
                for half in range(2):
                    ch = c * 2 + half
                    wb_, wk_ = load_w(wup[ch, :, :], KC * 128)
                    he, hk = hext[half], f"hext{half}"
                    if th == 0:
                        ph, phk = cx.pb()
                        for kc in range(KC):
                            mm(ph[:, 0:2], wb_[:, kc * 128:(kc + 1) * 128], hhT[:, kc, :], start=(kc == 0), stop=(kc == KC - 1), r=[wk_, "xnT"], w=[phk])
                        dve("tensor_copy", he[:, 0:2], ph[:, 0:2], r=[phk], w=[hk])
                    else:
                        dve("tensor_copy", he[:, 0:2], hprev[:, ch, :], r=["hprev"], w=[hk])
                    pu, puk = cx.pb()
                    for kc in range(KC):
                        mm(pu[:, 0:TH], wb_[:, kc * 128:(kc + 1) * 128], hnT[:, kc, :], start=(kc == 0), stop=(kc == KC - 1), r=[wk_, "xnT"], w=[puk])
                    act(he[:, 2:2 + TH], pu[:, 0:TH], AF.Copy, r=[puk], w=[hk])
                    dve("tensor_copy", hprev[:, ch, :], he[:, TH:TH + 2], r=[hk], w=["hprev"])
                    y, yk = (yg, "yg") if half == 0 else (yu, "yu")
                    dve("tensor_scalar", y[:], he[:, 2:2 + TH], fc[:, ch * 4 + 2:ch * 4 + 3], fc[:, ch * 4 + 3:ch * 4 + 4], ALU.mult, ALU.add,
                        r=[hk, "fc"], w=[yk])
                    dve("scalar_tensor_tensor", y[:], he[:, 1:1 + TH], fc[:, ch * 4 + 1:ch * 4 + 2], y[:], ALU.mult, ALU.add, r=[hk, "fc", yk], w=[yk])
                    dve("scalar_tensor_tensor", y[:], he[:, 0:TH], fc[:, ch * 4:ch * 4 + 1], y[:], ALU.mult, ALU.add, r=[hk, "fc", yk], w=[yk])
                act(yg[:], yg[:], AF.Silu, r=["yg"], w=["yg"])
                dve("tensor_tensor", actt[:, c, :], yg[:], yu[:], ALU.mult, r=["yg", "yu"], w=["actt"])
            for f in range(16):
                pd, pdk = cx.pb()
                for half in range(2):
                    wb_, wk_ = load_w(wdn[f * 2 + half, :, :], 22 * 128)
                    for k2 in range(22):
                        kc = half * 22 + k2
                        mm(pd[:, 0:TH], wb_[:, k2 * 128:(k2 + 1) * 128], actt[:, kc, :], start=(kc == 0), stop=(kc == NFC - 1), r=[wk_, "actt"], w=[pdk])
                cx.copy(yfm[:, f, :], pd[:, 0:TH], [pdk], ["yfm"])
            for s in range(TH // 128):
                r0 = t0 + s * 128
                xb, xk = xt[s % 2], f"xt{s % 2}"
                S.dma("sp", xk, [lambda e, xb=xb, r0=r0: e.dma_start(out=xb[:], in_=xm[2 + r0:2 + r0 + 128, :])], writes=[xk])
                for n in range(4):
                    pt, pk = cx.pb()
                    for j in range(4):
                        f = n * 4 + j
                        S.tr(pt[:, j * 128:(j + 1) * 128], yfm[:, f, s * 128:(s + 1) * 128], idf[:], reads=["yfm", "idf"], writes=[pk])
                    dve("tensor_tensor", xb[:, n * 512:(n + 1) * 512], xb[:, n * 512:(n + 1) * 512], pt[:, 0:512], ALU.add, r=[xk, pk], w=[xk])
                if final:
                    dve("memset", ss[:, 0:1], 0.0, w=["ss"])
                    act(junk[:], xb[:], AF.Square, accum_out=ss[:, 0:1], r=[xk, "ss"], w=["junk", "ss"])
                    act(rstd[:, 0:1], ss[:, 0:1], AF.Sqrt, scale=1.0 / D, bias=epsT[:, 0:1], r=["ss", "epsT"], w=["rstd"])
                    dve("reciprocal", rstd[:, 0:1], rstd[:, 0:1], r=["rstd"], w=["rstd"])
                    dve("scalar_tensor_tensor", xb[:], xb[:], rstd[:, 0:1], fg[:], ALU.mult, ALU.mult, r=[xk, "rstd", "fg"], w=[xk])
                S.dma("pool", f"xo{s % 2}", [lambda e, xb=xb, r0=r0: e.dma_start(out=xout[r0:r0 + 128, :], in_=xb[:])], reads=[xk], writes=["xout"])
        S.op("sp", None, reads=["xout"])
        S.emit()
    return nc

import numpy as np
import ml_dtypes

NCORES = 8
_BF = ml_dtypes.bfloat16
_PROGS = {}


def _prog(key, fn):
    if key not in _PROGS:
        _PROGS[key] = fn()
    return _PROGS[key]


def _cm(v, n):
    return np.ascontiguousarray(np.asarray(v, np.float32).reshape(n, 128).T)


def _run(nc, in_maps):
    res = run_bass_kernel_spmd(nc, in_maps, core_ids=list(range(len(in_maps))))
    return res.results


def kernel(x, positions, mix_norm, w_in, pool_w, pool_scale, gdn_conv_w, gdn_a_log, gdn_dt_bias, gdn_norm,
           conf_conv_w, conf_conv_b, conf_ln_g, conf_ln_b, mla_q_norm, mla_w_uq, mla_kv_norm, mla_w_ukv,
           w_pool_out, w_gdn_out, w_conf_out, w_mla_out, w_out, ffn_norm, ffn_w_up, ffn_conv_w, ffn_conv_b,
           ffn_w_down, final_norm):
    x = np.asarray(x, np.float32)
    B, S, D = x.shape
    NT = B * S
    T = NT // NCORES
    CPS = S // T
    depth = w_in.shape[0]
    pos_flat = np.ascontiguousarray(np.asarray(positions, np.int32).reshape(1, NT))
    xcur = x.reshape(NT, D)

    invf = (10000.0 ** (-np.arange(0, 64, 2, dtype=np.float32) / 64)).astype(np.float32)
    consts = np.zeros((128, 8), np.float32)
    consts[0:32, 0] = invf
    consts[32:64, 0] = invf
    consts[0:32, 1] = -1.0
    consts[32:64, 1] = 1.0
    ident = np.eye(128, dtype=np.float32)
    ii = np.arange(128)
    masks = np.zeros((128, 640), np.float32)
    masks[:, 0:128] = np.eye(128)
    masks[:, 128:256] = (ii[:, None] <= ii[None, :])
    masks[:, 256:384] = (ii[:, None] > ii[None, :])
    masks[:, 384:512] = (ii[None, :] >= ii[:, None])
    masks[:, 512:640] = (ii[None, :] > ii[:, None])

    ncA = _prog(("A", T), lambda: build_A(T))
    ncB = _prog(("B", S, B), lambda: build_B(S, B))
    ncC1 = _prog(("C1", T), lambda: build_C1(T))

    for l in range(depth):
        wl = np.asarray(w_in[l], np.float32)
        kr = wl[:, 7184:7248]
        commonA = dict(
            consts=consts, ident=ident, g_mix=_cm(mix_norm[l], 16), qn=_cm(mla_q_norm[l], 4), kvn=_cm(mla_kv_norm[l], 4),
            w_pool=np.ascontiguousarray(wl[:, 0:1024]), w_qkvz=np.ascontiguousarray(wl[:, 1024:4096]),
            w_ab=np.ascontiguousarray(wl[:, 4096:4112]), w_conf=np.ascontiguousarray(wl[:, 4112:6160]),
            w_cq=np.ascontiguousarray(wl[:, 6160:6672]), w_ckv=np.ascontiguousarray(wl[:, 6672:7184]),
            w_kr=np.ascontiguousarray(np.concatenate([kr, kr[:, 32:], kr[:, :32]], axis=1)))
        insA = []
        for c in range(NCORES):
            d = dict(commonA)
            d["x"] = np.ascontiguousarray(xcur[c * T:(c + 1) * T])
            d["pos"] = np.ascontiguousarray(pos_flat[:, c * T:(c + 1) * T])
            insA.append(d)
        rA = _run(ncA, insA)
        cat = lambda k: np.concatenate([r[k] for r in rA], axis=1)
        G_pool = cat("o_pool")
        G_qkvz = cat("o_qkvz")
        G_hglu = cat("o_hglu")
        G_cq = cat("o_cq")
        G_ckv = cat("o_ckv")
        G_kpe = cat("o_kpe")
        G_ab = np.concatenate([r["o_ab"] for r in rA], axis=0)
        del rA, insA
        cwf = np.asarray(gdn_conv_w[l], np.float32)
        insB = []
        for h in range(NCORES):
            kh = h // 2
            gcw = np.concatenate([cwf[:, kh * 128:(kh + 1) * 128].T, cwf[:, 512 + kh * 128:512 + (kh + 1) * 128].T,
                                  cwf[:, 1024 + h * 128:1024 + (h + 1) * 128].T], axis=1)
            gsc = np.zeros((128, 4), np.float32)
            gsc[:, 0] = gdn_a_log[l][h]
            gsc[:, 1] = gdn_dt_bias[l][h]
            gsc[:, 2] = gdn_norm[l]
            wuq = np.asarray(mla_w_uq[l][:, h * 192:(h + 1) * 192], np.float32)
            wq = np.concatenate([wuq[:, :128], wuq[:, 128:192], wuq[:, 160:192], wuq[:, 128:160]], axis=1)
            wkv = np.asarray(mla_w_ukv[l][:, h * 256:(h + 1) * 256], np.float32)
            insB.append(dict(
                gq=np.ascontiguousarray(G_qkvz[kh * 128:(kh + 1) * 128]),
                gk=np.ascontiguousarray(G_qkvz[512 + kh * 128:512 + (kh + 1) * 128]),
                gv=np.ascontiguousarray(G_qkvz[1024 + h * 128:1024 + (h + 1) * 128]),
                gz=np.ascontiguousarray(G_qkvz[2048 + h * 128:2048 + (h + 1) * 128]),
                gab=np.ascontiguousarray(np.stack([G_ab[:, h], G_ab[:, 8 + h]], axis=-1)),
                gcw=np.ascontiguousarray(gcw), gsc=gsc, masks=masks,
                cqn=G_cq, ckvn=G_ckv, kpe=G_kpe,
                wq=np.ascontiguousarray(wq), wkv=np.ascontiguousarray(wkv), pos=pos_flat, consts=consts))
        rB = _run(ncB, insB)
        OG = np.concatenate([r["o_gdn"] for r in rB], axis=0)
        OM = np.concatenate([r["o_mla"] for r in rB], axis=0)
        del rB, insB, G_qkvz, G_cq, G_ckv, G_kpe, G_ab
        Wg = np.ascontiguousarray(wl[:, 7248:].reshape(16, 128, 4, 16, 128).transpose(3, 2, 1, 0, 4)).reshape(64, 128, 2048)
        Wo4 = np.stack([np.asarray(w, np.float32) for w in (w_pool_out[l], w_gdn_out[l], w_conf_out[l], w_mla_out[l])], axis=0)
        Wo = np.ascontiguousarray(Wo4.reshape(4, 8, 128, 16, 128).transpose(3, 0, 2, 1, 4)).reshape(64, 128, 1024)
        Wout = np.ascontiguousarray(np.asarray(w_out[l], np.float32).reshape(16, 128, 16, 128).transpose(2, 1, 0, 3)).reshape(16, 128, 2048)
        poolw = np.ascontiguousarray(np.asarray(pool_w[l], np.float32).reshape(8, 128, 256))
        cconv = np.ascontiguousarray(np.asarray(conf_conv_w[l], np.float32).reshape(31, 8, 128).transpose(2, 1, 0)).reshape(128, 248)
        cvec0 = np.zeros((128, 64), np.float32)
        cvec0[:, 0:8] = _cm(pool_scale[l], 8)
        cvec0[:, 8:16] = _cm(conf_conv_b[l], 8)
        cvec0[:, 16:24] = _cm(conf_ln_g[l], 8)
        cvec0[:, 24:32] = _cm(conf_ln_b[l], 8)
        cvec0[:, 40:56] = np.arange(16, dtype=np.float32)[None, :]
        gmix = _cm(mix_norm[l], 16)
        insC = []
        for c in range(NCORES):
            q = c % CPS
            g0 = c * T
            cv = cvec0.copy()
            cv[:, 32] = q * T + 1
            up = np.zeros((1024, 32 + T), np.float32)
            hg = np.zeros((1024, 32 + T), np.float32)
            up[:, 32:] = G_pool[:, g0:g0 + T]
            hg[:, 32:] = G_hglu[:, g0:g0 + T]
            if q > 0:
                up[:, :32] = G_pool[:, g0 - 32:g0]
                hg[:, :32] = G_hglu[:, g0 - 32:g0]
            insC.append(dict(
                x=np.ascontiguousarray(xcur[g0:g0 + T]), upool=up, hglu=hg,
                ogdn=np.ascontiguousarray(OG[:, g0:g0 + T]), omla=np.ascontiguousarray(OM[:, g0:g0 + T]),
                ident=ident, g_mix=gmix, cvec=cv, cconv=cconv, poolw=poolw, wg=Wg, wo=Wo, wout=Wout))
        rC = _run(ncC1, insC)
        xmid = np.concatenate([r["xmid"] for r in rC], axis=0)
        del rC, insC, Wg, Wo, Wout, G_pool, G_hglu, OG, OM
        final = (l == depth - 1)
        ncC2 = _prog(("C2", T, final), lambda: build_C2(T, final))
        Wup = np.ascontiguousarray(np.asarray(ffn_w_up[l], np.float32).reshape(16, 128, 2, 44, 128).transpose(3, 2, 1, 0, 4)).reshape(88, 128, 2048)
        Wdn = np.ascontiguousarray(np.asarray(ffn_w_down[l], np.float32).reshape(2, 22, 128, 16, 128).transpose(3, 0, 2, 1, 4)).reshape(32, 128, 2816)
        fcw = np.asarray(ffn_conv_w[l], np.float32).reshape(3, 2, 44, 128).transpose(3, 2, 1, 0)
        fcb = np.asarray(ffn_conv_b[l], np.float32).reshape(2, 44, 128).transpose(2, 1, 0)[..., None]
        fcv = np.ascontiguousarray(np.concatenate([fcw, fcb], axis=-1)).reshape(128, 352)
        gffn = _cm(ffn_norm[l], 16)
        fng = np.ascontiguousarray(np.asarray(final_norm, np.float32).reshape(1, D))
        insD = []
        for c in range(NCORES):
            q = c % CPS
            g0 = c * T
            xm = np.zeros((2 + T, D), np.float32)
            xm[2:] = xmid[g0:g0 + T]
            if q > 0:
                xm[:2] = xmid[g0 - 2:g0]
            insD.append(dict(xm=xm, ident=ident, g_ffn=gffn, fcv=fcv, wup=Wup, wdn=Wdn, fng=fng))
        rD = _run(ncC2, insD)
        xcur = np.concatenate([r["xout"] for r in rD], axis=0)
        del rD, insD, Wup, Wdn
    return np.ascontiguousarray(xcur.reshape(B, S, D).astype(np.float32))
```
